# Optimizing a Trainium2 kernel written in Bass

```python
import jax, jax.numpy as jnp
from jax import lax
import numpy as np

D_MODEL = 1024
BATCH = 8
SEQ = 4096
DEPTH = 4
DEC_BATCH = 32
DEC_SEQ = 16
PAST_LEN = 2048

CHUNK = 64
N_META = 16
A_HEADS = 8
A_HEAD_DIM = 64
A_WIDTH = A_HEADS * A_HEAD_DIM
DECAY_LORA = 64
ICLR_LORA = 64
B_HEADS = 8
QK_NOPE = 64
QK_ROPE = 32
QK_DIM = QK_NOPE + QK_ROPE
V_DIM = 64
B_WIDTH = B_HEADS * V_DIM
Q_LORA = 384
KV_LORA = 256
ROPE_THETA = 10000.0
Q_BLOCK = 128
NORM_EPS = 1e-6
GN_EPS = 64e-5
NEG_INF = -1e30
A_COLS = 4 * A_WIDTH + DECAY_LORA + ICLR_LORA
B_COLS = Q_LORA + KV_LORA + QK_ROPE + B_WIDTH
G_COLS = 2 * D_MODEL
IN_COLS = A_COLS + B_COLS + G_COLS

kernel_name = 'rwkv7_mla_parallel_streaming_encoder'


def rmsnorm(x, g, eps=NORM_EPS):
    xf = x.astype(jnp.float32)
    y = xf * lax.rsqrt(jnp.mean(xf * xf, axis=-1, keepdims=True) + eps)
    return (y * g.astype(jnp.float32)).astype(x.dtype)


def rope(x, t):
    inv = ROPE_THETA ** (-jnp.arange(0, QK_ROPE, 2, dtype=jnp.float32) / QK_ROPE)
    ang = t.astype(jnp.float32)[:, None] * inv[None, :]
    if x.ndim == 4:
        ang = ang[:, None, :]
    cos = jnp.cos(ang).astype(x.dtype)
    sin = jnp.sin(ang).astype(x.dtype)
    x1, x2 = x[..., :QK_ROPE // 2], x[..., QK_ROPE // 2:]
    return jnp.concatenate([x1 * cos - x2 * sin, x1 * sin + x2 * cos], axis=-1)


def rwkv_scan(s0, r, decay, k, v, kk, a):
    def step(S, inp):
        rt, wt, kt, vt, kkt, at = inp
        sa = jnp.einsum('bhvk,bhk->bhv', S, -kkt)
        S = S * wt[:, :, None, :] + sa[..., None] * (kkt * at)[:, :, None, :] + vt[..., None] * kt[:, :, None, :]
        return S, jnp.einsum('bhvk,bhk->bhv', S, rt)
    xs = (jnp.moveaxis(r, 1, 0), jnp.moveaxis(decay, 1, 0), jnp.moveaxis(k, 1, 0),
          jnp.moveaxis(v, 1, 0), jnp.moveaxis(kk, 1, 0), jnp.moveaxis(a, 1, 0))
    s_fin, ys = lax.scan(step, s0, xs)
    return jnp.moveaxis(ys, 0, 1), s_fin


def rwkv_branch(a_cols, shift_row, s0, p):
    B, L, _ = a_cols.shape
    f32 = jnp.float32
    prev = jnp.concatenate([shift_row.astype(a_cols.dtype), a_cols[:, :-1]], axis=1)
    xs = a_cols * p['shift_mix'][0] + prev * p['shift_mix'][1]
    r, k, v, gate, wl, al = jnp.split(
        xs, [A_WIDTH, 2 * A_WIDTH, 3 * A_WIDTH, 4 * A_WIDTH, 4 * A_WIDTH + DECAY_LORA], axis=-1)
    w = -jax.nn.softplus(-(p['rwkv_w0'] + jnp.tanh(wl) @ p['rwkv_w2']).astype(f32)) - 0.5
    decay = jnp.exp(-jnp.exp(w))
    a = jax.nn.sigmoid((p['rwkv_a0'] + al @ p['rwkv_a2']).astype(f32))
    hd = lambda u: u.astype(f32).reshape(B, L, A_HEADS, A_HEAD_DIM)
    kk = hd(k * p['rwkv_k_k'])
    kk = kk * lax.rsqrt(jnp.maximum(jnp.sum(kk * kk, axis=-1, keepdims=True), 1e-24))
    k_mod = k.astype(f32) * (1.0 + (a - 1.0) * p['rwkv_k_a'].astype(f32))
    rh, kh, vh, ah, dh = hd(r), hd(k_mod), hd(v), hd(a), hd(decay)
    y, s_fin = rwkv_scan(s0.astype(f32), rh, dh, kh, vh, kk, ah)
    mu = jnp.mean(y, axis=-1, keepdims=True)
    var = jnp.mean(jnp.square(y - mu), axis=-1, keepdims=True)
    yn = ((y - mu) * lax.rsqrt(var + GN_EPS)).reshape(B, L, A_WIDTH) * p['rwkv_ln_w'] + p['rwkv_ln_b']
    bonus = (jnp.sum(rh * kh * p['rwkv_r_k'].astype(f32), axis=-1, keepdims=True) * vh).reshape(B, L, A_WIDTH)
    out = ((yn + bonus) * jax.nn.silu(gate.astype(f32))).astype(a_cols.dtype)
    return out, s_fin.astype(s0.dtype)


def mla_project(b_cols, t, p):
    B, L, _ = b_cols.shape
    qc, ckv, kr, gate = jnp.split(b_cols, [Q_LORA, Q_LORA + KV_LORA, Q_LORA + KV_LORA + QK_ROPE], axis=-1)
    q = (rmsnorm(qc, p['mla_q_norm']) @ p['mla_w_uq']).reshape(B, L, B_HEADS, QK_DIM)
    q = jnp.concatenate([rmsnorm(q[..., :QK_NOPE], p['mla_qn_nope']),
                         rope(rmsnorm(q[..., QK_NOPE:], p['mla_qn_rope']), t)], axis=-1)
    latent = rmsnorm(ckv, p['mla_kv_norm'])
    krope = rope(rmsnorm(kr, p['mla_kn_rope']), t)
    return q, latent, krope, gate


def mla_expand(latent, krope, p):
    B, L, _ = latent.shape
    kv = (latent @ p['mla_w_ukv']).reshape(B, L, B_HEADS, QK_NOPE + V_DIM)
    k_nope = rmsnorm(kv[..., :QK_NOPE], p['mla_kn_nope'])
    k = jnp.concatenate([k_nope, jnp.broadcast_to(krope[:, :, None, :], (B, L, B_HEADS, QK_ROPE))], axis=-1)
    return k, kv[..., QK_NOPE:]


def attn_chunk_causal(q, k, v, chunk_id):
    B, L = q.shape[:2]
    nb = -(-L // Q_BLOCK)
    Lp = nb * Q_BLOCK
    qp = jnp.pad(q, ((0, 0), (0, Lp - L), (0, 0), (0, 0)))
    qcid = jnp.pad(chunk_id, (0, Lp - L), mode='edge')
    qb = jnp.moveaxis(qp.reshape(B, nb, Q_BLOCK, B_HEADS, QK_DIM), 1, 0)
    cb = qcid.reshape(nb, Q_BLOCK)
    scale = QK_DIM ** -0.5

    def block(args):
        qi, ci = args
        s = jnp.einsum('bqhd,bkhd->bhqk', qi, k).astype(jnp.float32) * scale
        mask = chunk_id[None, :] <= ci[:, None]
        s = jnp.where(mask[None, None], s, NEG_INF)
        pr = jax.nn.softmax(s, axis=-1).astype(v.dtype)
        return jnp.einsum('bhqk,bkhd->bqhd', pr, v)

    o = lax.map(block, (qb, cb))
    return jnp.moveaxis(o, 0, 1).reshape(B, Lp, B_HEADS, V_DIM)[:, :L]


def attn_full(q, k, v):
    s = jnp.einsum('bqhd,bkhd->bhqk', q, k).astype(jnp.float32) * (QK_DIM ** -0.5)
    pr = jax.nn.softmax(s, axis=-1).astype(v.dtype)
    return jnp.einsum('bhqk,bkhd->bqhd', pr, v)


def mixer_layer(x, p, t, shift_row, s0, past_latent=None, past_krope=None, chunk_id=None):
    B, L, _ = x.shape
    h = rmsnorm(x, p['norm_w'])
    z = h @ p['w_in']
    a_cols, b_cols, g_cols = jnp.split(z, [A_COLS, A_COLS + B_COLS], axis=-1)
    ya, s_fin = rwkv_branch(a_cols, shift_row, s0, p)
    q, latent, krope, gate_b = mla_project(b_cols, t, p)
    if past_latent is None:
        k, v = mla_expand(latent, krope, p)
        o = attn_chunk_causal(q, k, v, chunk_id)
    else:
        k, v = mla_expand(jnp.concatenate([past_latent, latent], axis=1),
                          jnp.concatenate([past_krope, krope], axis=1), p)
        o = attn_full(q, k, v)
    yb = o.reshape(B, L, B_WIDTH) * jax.nn.silu(gate_b)
    ga, gb = jnp.split(jax.nn.sigmoid(g_cols), 2, axis=-1)
    y = (ga * (ya @ p['w_branch_a']) + gb * (yb @ p['w_branch_b'])) @ p['w_out']
    return x + y, s_fin, a_cols[:, -1:], latent, krope


def setup_inputs(seed: int = 0) -> dict:
    key = jax.random.key(seed)
    ks = jax.random.split(key, 32)
    f32 = jnp.float32
    nrm = lambda k, shape, s: jax.random.normal(k, shape, f32) * s
    mu = jax.random.uniform(ks[5], (DEPTH, A_COLS), f32)
    return {
        'x_prompt': nrm(ks[0], (BATCH, SEQ, D_MODEL), 1.0),
        'x_sample': nrm(ks[1], (DEC_BATCH, DEC_SEQ, D_MODEL), 1.0),
        'state_rwkv': nrm(ks[2], (DEPTH, DEC_BATCH, A_HEADS, A_HEAD_DIM, A_HEAD_DIM), 0.3),
        'state_shift': nrm(ks[3], (DEPTH, DEC_BATCH, 1, A_COLS), 1.0),
        'cache_mla_latent': nrm(ks[4], (DEPTH, DEC_BATCH, PAST_LEN, KV_LORA), 1.0),
        'cache_mla_krope': nrm(ks[6], (DEPTH, DEC_BATCH, PAST_LEN, QK_ROPE), 1.0),
        'meta_tokens': nrm(ks[7], (N_META, D_MODEL), 1.0),
        'norm_w': 1.0 + nrm(ks[8], (DEPTH, D_MODEL), 0.02),
        'w_in': nrm(ks[9], (DEPTH, D_MODEL, IN_COLS), D_MODEL ** -0.5),
        'shift_mix': jnp.stack([1.0 - mu, mu], axis=1),
        'rwkv_w0': jax.random.uniform(ks[10], (DEPTH, A_WIDTH), f32, -6.0, 0.0),
        'rwkv_w2': nrm(ks[11], (DEPTH, DECAY_LORA, A_WIDTH), 0.5 * DECAY_LORA ** -0.5),
        'rwkv_a0': nrm(ks[12], (DEPTH, A_WIDTH), 0.1),
        'rwkv_a2': nrm(ks[13], (DEPTH, ICLR_LORA, A_WIDTH), 0.5 * ICLR_LORA ** -0.5),
        'rwkv_k_k': 0.85 + nrm(ks[14], (DEPTH, A_WIDTH), 0.05),
        'rwkv_k_a': 1.0 + nrm(ks[15], (DEPTH, A_WIDTH), 0.05),
        'rwkv_r_k': nrm(ks[16], (DEPTH, A_HEADS, A_HEAD_DIM), 0.1),
        'rwkv_ln_w': 1.0 + nrm(ks[17], (DEPTH, A_WIDTH), 0.02),
        'rwkv_ln_b': nrm(ks[18], (DEPTH, A_WIDTH), 0.02),
        'mla_q_norm': 1.0 + nrm(ks[19], (DEPTH, Q_LORA), 0.02),
        'mla_w_uq': nrm(ks[20], (DEPTH, Q_LORA, B_HEADS * QK_DIM), Q_LORA ** -0.5),
        'mla_kv_norm': 1.0 + nrm(ks[21], (DEPTH, KV_LORA), 0.02),
        'mla_w_ukv': nrm(ks[22], (DEPTH, KV_LORA, B_HEADS * (QK_NOPE + V_DIM)), KV_LORA ** -0.5),
        'mla_qn_nope': 1.0 + nrm(ks[23], (DEPTH, QK_NOPE), 0.02),
        'mla_kn_nope': 1.0 + nrm(ks[24], (DEPTH, QK_NOPE), 0.02),
        'mla_qn_rope': 1.0 + nrm(ks[25], (DEPTH, QK_ROPE), 0.02),
        'mla_kn_rope': 1.0 + nrm(ks[26], (DEPTH, QK_ROPE), 0.02),
        'w_branch_a': nrm(ks[27], (DEPTH, A_WIDTH, D_MODEL), A_WIDTH ** -0.5),
        'w_branch_b': nrm(ks[28], (DEPTH, B_WIDTH, D_MODEL), B_WIDTH ** -0.5),
        'w_out': nrm(ks[29], (DEPTH, D_MODEL, D_MODEL), D_MODEL ** -0.5),
    }


def reference(x_prompt, x_sample, state_rwkv, state_shift, cache_mla_latent, cache_mla_krope,
              meta_tokens, norm_w, w_in, shift_mix, rwkv_w0, rwkv_w2, rwkv_a0, rwkv_a2,
              rwkv_k_k, rwkv_k_a, rwkv_r_k, rwkv_ln_w, rwkv_ln_b, mla_q_norm, mla_w_uq,
              mla_kv_norm, mla_w_ukv, mla_qn_nope, mla_kn_nope, mla_qn_rope, mla_kn_rope,
              w_branch_a, w_branch_b, w_out):
    bp, seq = x_prompt.shape[:2]
    past = cache_mla_latent.shape[2]
    meta = jnp.broadcast_to(meta_tokens[None].astype(x_prompt.dtype), (bp, N_META, D_MODEL))
    xp = jnp.concatenate([meta, x_prompt], axis=1)
    t_p = jnp.arange(-N_META, seq, dtype=jnp.int32)
    cid = jnp.where(t_p < 0, -1, t_p // CHUNK)
    xs = x_sample
    t_s = past + jnp.arange(x_sample.shape[1], dtype=jnp.int32)
    sp_list, shp_list, latp_list, krp_list = [], [], [], []
    ss_list, shs_list, lats_list, krs_list = [], [], [], []
    for l in range(DEPTH):
        p = {'norm_w': norm_w[l], 'w_in': w_in[l], 'shift_mix': shift_mix[l],
             'rwkv_w0': rwkv_w0[l], 'rwkv_w2': rwkv_w2[l], 'rwkv_a0': rwkv_a0[l], 'rwkv_a2': rwkv_a2[l],
             'rwkv_k_k': rwkv_k_k[l], 'rwkv_k_a': rwkv_k_a[l], 'rwkv_r_k': rwkv_r_k[l],
             'rwkv_ln_w': rwkv_ln_w[l], 'rwkv_ln_b': rwkv_ln_b[l],
             'mla_q_norm': mla_q_norm[l], 'mla_w_uq': mla_w_uq[l], 'mla_kv_norm': mla_kv_norm[l],
             'mla_w_ukv': mla_w_ukv[l], 'mla_qn_nope': mla_qn_nope[l], 'mla_kn_nope': mla_kn_nope[l],
             'mla_qn_rope': mla_qn_rope[l], 'mla_kn_rope': mla_kn_rope[l],
             'w_branch_a': w_branch_a[l], 'w_branch_b': w_branch_b[l], 'w_out': w_out[l]}
        zero_row = jnp.zeros((bp, 1, A_COLS), xp.dtype)
        zero_state = jnp.zeros((bp, A_HEADS, A_HEAD_DIM, A_HEAD_DIM), jnp.float32)
        xp, s_p, sh_p, lat_p, kr_p = mixer_layer(xp, p, t_p, zero_row, zero_state, chunk_id=cid)
        xs, s_s, sh_s, lat_s, kr_s = mixer_layer(xs, p, t_s, state_shift[l], state_rwkv[l],
                                                 cache_mla_latent[l], cache_mla_krope[l])
        sp_list.append(s_p); shp_list.append(sh_p); latp_list.append(lat_p); krp_list.append(kr_p)
        ss_list.append(s_s); shs_list.append(sh_s); lats_list.append(lat_s); krs_list.append(kr_s)
    y_prompt = xp[:, N_META:]
    y_sample = xs
    state_rwkv_prompt = jnp.stack(sp_list)
    state_shift_prompt = jnp.stack(shp_list)
    cache_mla_latent_prompt = jnp.stack(latp_list)
    cache_mla_krope_prompt = jnp.stack(krp_list)
    state_rwkv_sample = jnp.stack(ss_list)
    state_shift_sample = jnp.stack(shs_list)
    cache_mla_latent_new = jnp.stack(lats_list)
    cache_mla_krope_new = jnp.stack(krs_list)
    return (y_prompt, y_sample, state_rwkv_prompt, state_shift_prompt, cache_mla_latent_prompt,
            cache_mla_krope_prompt, state_rwkv_sample, state_shift_sample, cache_mla_latent_new,
            cache_mla_krope_new)
```

```python
import contextlib
import math
import numpy as np
import concourse.bass as bass
import concourse.mybir as mybir
from concourse.bass_utils import run_bass_kernel_spmd

F32 = mybir.dt.float32
BF16 = mybir.dt.bfloat16
AF = mybir.ActivationFunctionType
ALU = mybir.AluOpType
AX = mybir.AxisListType

ENGINES = ("tensor", "vector", "scalar", "gpsimd", "sync")


class StopBuild(Exception):
    pass


class Sched:
    def __init__(self, n_dma_sems=32):
        self.ops = []
        self.last_w = {}
        self.readers = {}
        self.n_dma_sems = n_dma_sems

    @staticmethod
    def _flat(keys):
        out = []
        for k in keys:
            if isinstance(k, tuple):
                out.extend(k)
            else:
                out.append(k)
        return out

    def _add(self, engine, fn, reads, writes, is_dma):
        reads = self._flat(reads)
        writes = self._flat(writes)
        idx = len(self.ops)
        deps = set()
        for k in reads:
            w = self.last_w.get(k)
            if w is not None:
                deps.add(w)
        for k in writes:
            w = self.last_w.get(k)
            if w is not None:
                deps.add(w)
            deps.update(self.readers.get(k, ()))
        for k in reads:
            self.readers.setdefault(k, []).append(idx)
        for k in writes:
            self.last_w[k] = idx
            self.readers[k] = []
        self.ops.append((engine, fn, deps, is_dma))
        return idx

    def op(self, engine, fn, reads=(), writes=()):
        return self._add(engine, fn, tuple(reads), tuple(writes), False)

    def dma(self, queue, out, in_, reads=(), writes=(), slow=False):
        if slow:
            fn = lambda eng: eng.dma_start(out=out, in_=in_, allow_slow_non_contiguous=True)
        else:
            fn = lambda eng: eng.dma_start(out=out, in_=in_)
        if slow:
            writes = tuple(writes) + ("__slowdma",)
        return self._add(queue, fn, tuple(reads), tuple(writes), True)

    def emit(self, nc, final_wait_engine="sync"):
        ops = self.ops
        n = len(ops)
        needed = [False] * n
        for o in ops:
            for d in o[2]:
                needed[d] = True
        eng_count = {e: 0 for e in ENGINES}
        dma_count = [0] * self.n_dma_sems
        token = [None] * n
        pre_wait = [None] * n
        rr = 0
        for i, o in enumerate(ops):
            if o[3]:
                s = rr % self.n_dma_sems
                rr += 1
                if dma_count[s] > 0:
                    pre_wait[i] = (("dma", s), dma_count[s] * 16)
                dma_count[s] += 1
                token[i] = (("dma", s), dma_count[s] * 16)
            elif needed[i]:
                e = o[0]
                eng_count[e] += 1
                token[i] = (("eng", e), eng_count[e])
        with contextlib.ExitStack() as es:
            sems = {}
            for e in ENGINES:
                sems[("eng", e)] = es.enter_context(nc.semaphore("s_" + e))
            for s in range(self.n_dma_sems):
                sems[("dma", s)] = es.enter_context(nc.semaphore("s_dma%d" % s))
            block = es.enter_context(nc.Block())
            streams = {e: [] for e in ENGINES}
            for i, o in enumerate(ops):
                streams[o[0]].append(i)
            final_tokens = {}
            for i, o in enumerate(ops):
                if o[3]:
                    k, v = token[i]
                    final_tokens[k] = max(final_tokens.get(k, 0), v)

            self.stats = {}

            def run(engname, eng):
                seen = {}
                nw = 0
                for i in streams[engname]:
                    o = ops[i]
                    waits = {}
                    if pre_wait[i] is not None:
                        k, v = pre_wait[i]
                        waits[k] = v
                    for d in o[2]:
                        k, v = token[d]
                        if engname == "tensor" and k == ("eng", "tensor"):
                            continue
                        if waits.get(k, 0) < v:
                            waits[k] = v
                    for k, v in waits.items():
                        if seen.get(k, 0) >= v:
                            continue
                        eng.wait_ge(sems[k], v)
                        seen[k] = v
                        nw += 1
                    ins = o[1](eng)
                    if token[i] is not None:
                        ins.then_inc(sems[token[i][0]], 16 if o[3] else 1)
                self.stats[engname] = (len(streams[engname]), nw, dict(eng_count), max(dma_count))
                if engname == final_wait_engine:
                    for k, v in final_tokens.items():
                        if seen.get(k, 0) < v:
                            eng.wait_ge(sems[k], v)

            block.tensor(lambda e: run("tensor", e))
            block.vector(lambda e: run("vector", e))
            block.scalar(lambda e: run("scalar", e))
            block.gpsimd(lambda e: run("gpsimd", e))
            block.sync(lambda e: run("sync", e))
        return n


D = 1024
NMETA = 16
C0 = math.exp(-0.5)
NORM_EPS = 1e-6
GN_EPS = 64e-5
NAB = 3424
NG = 2048
OFF_QC, OFF_CKV, OFF_KR, OFF_GB = 2176, 2560, 2816, 2912


def a_src(ci):
    if ci == 0:
        return 2048
    hp, j = divmod(ci - 1, 4)
    return j * 512 + hp * 128


VEC_NAMES = ([("mix0", c) for c in range(17)] + [("mix1", c) for c in range(17)] +
             [(nm, hp) for nm in ("w0", "a0", "kk", "ka", "rk", "lnw", "lnb") for hp in range(4)] +
             [("qnorm", j) for j in range(3)] + [("kvnorm", j) for j in range(2)] +
             [("qn", 0), ("knn", 0), ("knr", 0)])
VIDX = {k: i for i, k in enumerate(VEC_NAMES)}
NV = len(VEC_NAMES)


def host_layout(inp, cfg):
    DEPTH, SEQ, PAST = cfg["DEPTH"], cfg["SEQ"], cfg["PAST"]
    f = np.float32
    w_in = np.asarray(inp["w_in"], f)
    win_ab = np.zeros((DEPTH, D, NAB), f)
    for ci in range(17):
        s = a_src(ci)
        win_ab[:, :, ci * 128:(ci + 1) * 128] = w_in[:, :, s:s + 128]
    win_ab[:, :, OFF_QC:OFF_QC + 384] = w_in[:, :, 2176:2560]
    win_ab[:, :, OFF_CKV:OFF_CKV + 256] = w_in[:, :, 2560:2816]
    win_ab[:, :, OFF_KR + 64:OFF_KR + 96] = w_in[:, :, 2816:2848]
    win_ab[:, :, OFF_GB:OFF_GB + 512] = w_in[:, :, 2848:3360]
    win_g = np.zeros((DEPTH, D, NG), f)
    for j in range(8):
        win_g[:, :, (2 * j) * 128:(2 * j + 1) * 128] = w_in[:, :, 3360 + j * 128:3360 + (j + 1) * 128]
        win_g[:, :, (2 * j + 1) * 128:(2 * j + 2) * 128] = w_in[:, :, 4384 + j * 128:4384 + (j + 1) * 128]
    wukv = np.asarray(inp["mla_w_ukv"], f).reshape(DEPTH, 256, 8, 128)
    wuk = np.ascontiguousarray(wukv[:, :, :, :64]).reshape(DEPTH, 256, 512)
    wuv = np.ascontiguousarray(wukv[:, :, :, 64:]).reshape(DEPTH, 256, 512)
    w2a2 = np.concatenate([np.asarray(inp["rwkv_w2"], f), np.asarray(inp["rwkv_a2"], f)], axis=1)
    vecs = np.zeros((DEPTH, 128, NV), f)
    sm = np.asarray(inp["shift_mix"], f)
    for ci in range(17):
        s = a_src(ci)
        vecs[:, :, VIDX[("mix0", ci)]] = sm[:, 0, s:s + 128]
        vecs[:, :, VIDX[("mix1", ci)]] = sm[:, 1, s:s + 128]
    for nm, key in (("w0", "rwkv_w0"), ("a0", "rwkv_a0"), ("kk", "rwkv_k_k"), ("ka", "rwkv_k_a"),
                    ("rk", "rwkv_r_k"), ("lnw", "rwkv_ln_w"), ("lnb", "rwkv_ln_b")):
        v = np.asarray(inp[key], f).reshape(DEPTH, 512)
        for hp in range(4):
            vecs[:, :, VIDX[(nm, hp)]] = v[:, hp * 128:(hp + 1) * 128]
    for j in range(3):
        vecs[:, :, VIDX[("qnorm", j)]] = np.asarray(inp["mla_q_norm"], f)[:, j * 128:(j + 1) * 128]
    for j in range(2):
        vecs[:, :, VIDX[("kvnorm", j)]] = np.asarray(inp["mla_kv_norm"], f)[:, j * 128:(j + 1) * 128]
    vecs[:, 0:64, VIDX[("qn", 0)]] = np.asarray(inp["mla_qn_nope"], f)
    vecs[:, 64:96, VIDX[("qn", 0)]] = np.asarray(inp["mla_qn_rope"], f)
    vecs[:, 0:64, VIDX[("knn", 0)]] = np.asarray(inp["mla_kn_nope"], f)
    vecs[:, 64:96, VIDX[("knr", 0)]] = np.asarray(inp["mla_kn_rope"], f)
    normw = np.ascontiguousarray(np.broadcast_to(np.asarray(inp["norm_w"], f)[:, None, :], (DEPTH, 128, D)))

    ident = np.eye(128, dtype=f)
    ob1 = np.zeros((128, 128), f); ob1[:64, :64] = 1; ob1[64:, 64:] = 1
    oq = np.zeros((128, 128), f); oq[:64, :64] = 1 / 64; oq[64:96, 64:96] = 1 / 32
    rot = np.zeros((128, 128), f)
    for i in range(16):
        rot[64 + i + 16, 64 + i] = -1.0
        rot[64 + i, 64 + i + 16] = 1.0
    cmat = np.stack([ident, ob1, ob1 / 64.0, np.full((128, 128), 1 / 384, f), np.full((128, 128), 1 / 256, f),
                     oq, rot]).astype(f)
    pi = np.arange(128)[:, None]
    fi = np.arange(128)[None, :]
    tri = np.stack([(pi < fi), (pi > fi), (pi <= fi)]).astype(f)
    tri = np.ascontiguousarray(np.broadcast_to(tri[:, :, None, :], (3, 128, 2, 128))).reshape(3, 128, 256)
    amask = ((pi // 64) <= (fi // 64)).astype(f)
    inv = 10000.0 ** (-np.arange(0, 32, 2, dtype=np.float32) / 32)

    def cs_table(pos):
        ang = pos.astype(np.float32)[:, None] * inv[None, :]
        c = np.cos(ang).astype(f).T
        s = np.sin(ang).astype(f).T
        n = pos.shape[0]
        tab = np.zeros((2, 128, n), f)
        tab[0, :64] = 1.0
        tab[0, 64:80] = c; tab[0, 80:96] = c
        tab[1, 64:80] = s; tab[1, 80:96] = s
        return tab

    cs_p = cs_table(np.arange(-NMETA, SEQ))
    cs_s = cs_table(np.tile(PAST + np.arange(16), 4))
    common = dict(win_ab=win_ab, win_g=win_g, wuq=np.asarray(inp["mla_w_uq"], f), wuk=wuk, wuv=wuv,
                  wa=np.asarray(inp["w_branch_a"], f), wb=np.asarray(inp["w_branch_b"], f),
                  wout=np.asarray(inp["w_out"], f), w2a2=w2a2, vecs=vecs, normw=normw,
                  meta=np.asarray(inp["meta_tokens"], f), cmat=cmat, tri=tri, amask=amask,
                  cs_p=cs_p, cs_s=cs_s)
    maps = []
    for c in range(8):
        m = dict(common)
        m["xp"] = np.ascontiguousarray(np.asarray(inp["x_prompt"], f)[c])
        sl = slice(4 * c, 4 * c + 4)
        m["xs"] = np.ascontiguousarray(np.asarray(inp["x_sample"], f)[sl]).reshape(64, D)
        m["st_rwkv"] = np.ascontiguousarray(np.asarray(inp["state_rwkv"], f)[:, sl])
        m["st_shift"] = np.ascontiguousarray(np.asarray(inp["state_shift"], f)[:, sl, 0, :])
        m["c_lat"] = np.ascontiguousarray(np.asarray(inp["cache_mla_latent"], f)[:, sl])
        m["c_kr"] = np.ascontiguousarray(np.asarray(inp["cache_mla_krope"], f)[:, sl])
        maps.append(m)
    return maps


def build(cfg):
    DEPTH, SEQ, PAST = cfg["DEPTH"], cfg["SEQ"], cfg["PAST"]
    L = NMETA + SEQ
    NPT = SEQ // 128
    NKT_P = 1 + NPT
    NKT_S = PAST // 128 + 1
    LK = max(L, PAST + 16)
    NKT = max(NKT_P, NKT_S)
    nc = bass.Bass("TRN2", target_bir_lowering=False)
    S = Sched()
    es = contextlib.ExitStack()

    STOP = cfg.get("STOP", 99)

    AT = cfg.get("AT")
    cur = [0, 0, 0]

    def chk(level):
        if STOP == level and (AT is None or 100 <= level < 400 or tuple(cur[:len(AT)]) == tuple(AT)):
            raise StopBuild()

    def din(name, shape):
        return nc.dram_tensor(name, list(shape), F32, kind="ExternalInput").ap()

    def dout(name, shape):
        return nc.dram_tensor(name, list(shape), F32, kind="ExternalOutput").ap()

    def dscr(name, shape, dt):
        return nc.dram_tensor(name, list(shape), dt, kind="Internal").ap()

    i_win_ab = din("win_ab", [DEPTH, D, NAB]); i_win_g = din("win_g", [DEPTH, D, NG])
    i_wuq = din("wuq", [DEPTH, 384, 768]); i_wuk = din("wuk", [DEPTH, 256, 512]); i_wuv = din("wuv", [DEPTH, 256, 512])
    i_wa = din("wa", [DEPTH, 512, D]); i_wb = din("wb", [DEPTH, 512, D]); i_wout = din("wout", [DEPTH, D, D])
    i_w2a2 = din("w2a2", [DEPTH, 128, 512]); i_vecs = din("vecs", [DEPTH, 128, NV]); i_normw = din("normw", [DEPTH, 128, D])
    i_meta = din("meta", [NMETA, D]); i_cmat = din("cmat", [7, 128, 128]); i_tri = din("tri", [3, 128, 256])
    i_amask = din("amask", [128, 128]); i_csp = din("cs_p", [2, 128, L]); i_css = din("cs_s", [2, 128, 64])
    i_xp = din("xp", [SEQ, D]); i_xs = din("xs", [64, D])
    i_strw = din("st_rwkv", [DEPTH, 4, 8, 64, 64]); i_stsh = din("st_shift", [DEPTH, 4, 2176])
    i_clat = din("c_lat", [DEPTH, 4, PAST, 256]); i_ckr = din("c_kr", [DEPTH, 4, PAST, 32])

    o_yp = dout("y_p", [SEQ, D]); o_ys = dout("y_s", [64, D])
    o_srp = dout("o_srp", [DEPTH, 8, 64, 64]); o_shp = dout("o_shp", [DEPTH, 2176])
    o_latp = dout("o_latp", [DEPTH, L, 256]); o_krp = dout("o_krp", [DEPTH, L, 32])
    o_srs = dout("o_srs", [DEPTH, 4, 8, 64, 64]); o_shs = dout("o_shs", [DEPTH, 4, 2176])
    o_latn = dout("o_latn", [DEPTH, 64, 256]); o_krn = dout("o_krn", [DEPTH, 64, 32])

    d_xres = dscr("d_xres", [L + 64, D], F32)
    d_hT = dscr("d_hT", [128, 8, L + 64], BF16)
    d_oa = dscr("d_oa", [128, 4, L + 64], BF16)
    d_yb = dscr("d_yb", [64, 8, L + 64], BF16)
    d_kS = dscr("d_kS", [5, 8, 96, LK], BF16)
    d_vS = dscr("d_vS", [5, 8, 128, NKT, 65], BF16)
    wb_win_ab = dscr("b_win_ab", [DEPTH, D, NAB], BF16); wb_win_g = dscr("b_win_g", [DEPTH, D, NG], BF16)
    wb_wuq = dscr("b_wuq", [DEPTH, 384, 768], BF16); wb_wuk = dscr("b_wuk", [DEPTH, 256, 512], BF16)
    wb_wuv = dscr("b_wuv", [DEPTH, 256, 512], BF16)
    wb_wa = dscr("b_wa", [DEPTH, 512, D], BF16); wb_wb = dscr("b_wb", [DEPTH, 512, D], BF16)
    wb_wout = dscr("b_wout", [DEPTH, D, D], BF16)

    def sb(name, shape, dt=F32):
        return es.enter_context(nc.sbuf_tensor(name, list(shape), dt))

    WAR = sb("WAR", [128, 36864], BF16)
    off = 0
    def wview(n):
        nonlocal off
        v = WAR[:, off:off + n]
        off += n
        return v
    WIN = wview(8 * NAB).rearrange("p (k c) -> p k c", k=8)
    WUQ = wview(3 * 768).rearrange("p (k c) -> p k c", k=3)
    WUK = wview(2 * 512).rearrange("p (k c) -> p k c", k=2)
    WUV = wview(2 * 512).rearrange("p (k c) -> p k c", k=2)
    assert off <= 36864
    off = 0
    WING = wview(8 * NG).rearrange("p (k c) -> p k c", k=8)
    WA = wview(4 * D).rearrange("p (k c) -> p k c", k=4)
    WB = wview(8 * D).rearrange("p (k c) -> p k c", k=8)
    WOUT = wview(8 * D).rearrange("p (k c) -> p k c", k=8)
    assert off <= 36864

    CM = sb("CM", [128, 7, 128])
    IDENT, OB1, OB64, O384, O256, OQ, ROT = [CM[:, i, :] for i in range(7)]
    IDB = sb("IDB", [128, 128], BF16)
    TRI = sb("TRI", [128, 3, 256])
    AMK = sb("AMK", [128, 128], BF16)
    RMK = sb("RMK", [128, 128])
    ONE65 = sb("ONE65", [65, 64])
    W2A2 = sb("W2A2", [128, 512])
    VEC = sb("VEC", [128, NV])
    NRW = sb("NRW", [128, D])
    XT = sb("XT", [128, 2, D])
    SSQ = sb("SSQ", [128, 4])
    EPSV = sb("EPSV", [128, 4])
    JNK = sb("JNK", [128, D], BF16)
    HB = sb("HB", [128, D], BF16)
    HT = sb("HT", [128, 2, 8, 128], BF16)
    CST = sb("CST", [128, 2, 128])
    CAR = sb("CAR", [128, 17, 4])
    PBF = sb("PBF", [128, 132])
    XW = sb("XW", [128, 128]); TH = sb("TH", [128, 128])
    RW = {nm: sb("RW_" + nm, [128, 128]) for nm in
          ("XR", "XK", "XV", "XG", "SG", "CS", "AI", "KKN", "KMOD", "EP", "EM", "EE", "AT", "BT", "KT", "RT",
           "BG", "KG", "T1", "T2", "BON", "RH", "Y0", "YY", "YC", "GS")}
    NB = sb("NB", [128, 8]); GC = sb("GC", [128, 8])
    TM = {nm: sb("TM_" + nm, [128, 128]) for nm in ("A", "V", "BG", "KG")}
    MX = {nm: sb("MX_" + nm, [128, 256]) for nm in ("NT", "N", "AAK", "ARB", "ARK", "P0", "Q0", "P1", "Q1", "X0", "X1")}
    ST = sb("ST", [128, 5, 4, 64])
    MT = sb("MT", [128, 64]); NM = sb("NM", [128, 64])
    OUTA = sb("OUTA", [128, 4, 128], BF16)
    QC = sb("QC", [128, 3, 128]); QCN = sb("QCN", [128, 3, 128], BF16)
    CKV = sb("CKV", [128, 2, 128]); LAT = sb("LAT", [128, 2, 128]); LATB = sb("LATB", [128, 2, 128], BF16)
    LTM = sb("LTM", [128, 256])
    KR = sb("KR", [128, 128]); KRN = sb("KRN", [128, 128]); KRF = sb("KRF", [128, 128]); KTM = sb("KTM", [128, 96])
    SGB = sb("SGB", [64, 8, 128])
    QH = sb("QH", [96, 128]); QN = sb("QN", [96, 128]); QF = sb("QF", [96, 8, 128], BF16)
    KH = sb("KH", [64, 128]); KTB = sb("KTB", [96, 8, 128], BF16)
    VTB = sb("VTB", [128, 8, 65], BF16)
    KSB = sb("KSB", [96, 2, LK], BF16)
    VSB = sb("VSB", [128, 2, NKT, 65], BF16)
    EXB = sb("EXB", [128, 4, 128], BF16)
    OSB = sb("OSB", [65, 128]); RCP = sb("RCP", [64, 128]); OT = sb("OT", [64, 128])
    YB = sb("YB", [64, 8, 128], BF16)
    GAB = sb("GAB", [128, 2, 128])
    MM = sb("MM", [128, 8, 128], BF16)
    T3 = sb("T3", [128, 128]); T4 = sb("T4", [128, 128])
    PLT = sb("PLT", [128, 4, 256])
    PKT = sb("PKT", [128, 4, 96])
    LPB = sb("LPB", [128, 2, 512], BF16)
    KPF = sb("KPF", [96, 512], BF16)

    PSALL = es.enter_context(nc.psum_tensor("psall", [128, 7 * 512], F32))
    PSB = [PSALL[:, i * 512:(i + 1) * 512] for i in range(7)]
    PSPAIR = PSALL[:, 4 * 512:6 * 512].rearrange("p (h c) -> p h c", h=2)
    PTB = es.enter_context(nc.psum_tensor("ptb", [128, 1024], BF16))
    bank_rr = [0]
    pair_rr = [0]
    acc_rr = [0]

    def psq():
        b = bank_rr[0] % 5
        bank_rr[0] += 1
        return PSB[b][:, 0:128], "B%d" % b

    def psb_full():
        b = bank_rr[0] % 5
        bank_rr[0] += 1
        return PSB[b][:, :], "B%d" % b

    def psh():
        i = pair_rr[0] % 2
        pair_rr[0] += 1
        v = PSALL[:, (2 * i) * 512:(2 * i + 2) * 512].rearrange("p (h c) -> p h c", h=2)[:, :, 0:128]
        return v, ("B%d" % (2 * i), "B%d" % (2 * i + 1))

    def psacc():
        b = 5 + acc_rr[0] % 2
        acc_rr[0] += 1
        return PSB[b][:, 0:128], "B%d" % b

    F32R = mybir.dt.float32r

    def mm(out, lhsT, rhs, r, w, start=True, stop=True, **kw):
        S.op("tensor", lambda e: e.matmul(out, lhsT, rhs, start=start, stop=stop, **kw), r, w)

    def tr(out, in_, ident, r, w):
        S.op("tensor", lambda e: e.transpose(out, in_, ident), r, w)

    def act(out, in_, func, r, w, **kw):
        S.op("scalar", lambda e: e.activation(out, in_, func, **kw), r, w)

    def tt(out, a, b, op, r, w, eng="vector"):
        S.op(eng, lambda e: e.tensor_tensor(out, a, b, op), r, w)

    def ts(out, a, s1, s2, op0, op1, r, w, eng="vector"):
        if op1 is None:
            S.op(eng, lambda e: e.tensor_scalar(out, a, s1, None, op0), r, w)
        else:
            S.op(eng, lambda e: e.tensor_scalar(out, a, s1, s2, op0, op1), r, w)

    def stt(out, a, s, b, op0, op1, r, w, eng="vector"):
        S.op(eng, lambda e: e.scalar_tensor_tensor(out, a, s, b, op0, op1), r, w)

    def rsq(out, in_, epsi, r, w, scale=1.0):
        np_ = out.shape[0]
        bp = out.base_partition()
        S.op("scalar", lambda e: e.activation(out, in_, AF.Sqrt, bias=EPSV[bp:bp + np_, epsi:epsi + 1], scale=scale),
             list(r) + ["EPSV"], w)
        S.op("vector", lambda e: e.reciprocal(out, out), w, w)

    def cp(out, in_, r, w, eng="vector"):
        if eng == "scalar":
            S.op("scalar", lambda e: e.copy(out, in_), r, w)
        else:
            S.op(eng, lambda e: e.tensor_copy(out, in_), r, w)

    def ms(ap, val, w, eng="vector"):
        S.op(eng, lambda e: e.memset(ap, val), (), w)

    def vec(l_unused, nm, i=0, lo=0, hi=128):
        c = VIDX[(nm, i)]
        return VEC[lo:hi, c:c + 1]

    S.dma("sync", CM[:], i_cmat.rearrange("n p c -> p n c"), writes=["CM"])
    S.dma("sync", TRI[:], i_tri.rearrange("n p c -> p n c"), writes=["TRI"])
    S.dma("sync", T3[:], i_amask, writes=["T3"])
    cp(AMK[:], T3[:], ["T3"], ["AMK"])
    cp(IDB[:], IDENT, ["CM"], ["IDB"])
    ms(RMK[:], 1.0, ["RMK"])
    ms(RMK[:, 0:1], 0.0, ["RMK"])
    ms(ONE65[:], 1.0, ["ONE65"])
    ms(VSB[:], 1.0, ["VSB0", "VSB1"])
    ms(VTB[:], 1.0, ["VTB"])
    ms(PKT[:], 0.0, ["PKT"])
    ms(KRF[:], 0.0, ["KRF"])
    ms(SSQ[:], 0.0, ["SSQ"])
    ms(EPSV[:, 0:1], NORM_EPS, ["EPSV"])
    ms(EPSV[:, 1:2], GN_EPS, ["EPSV"])
    ms(EPSV[:, 2:3], 1e-24, ["EPSV"])

    cast_rr = [0]

    def cast_weight(src, dst, rows, cols):
        for r0 in range(0, rows, 128):
            for c0 in range(0, cols, 1024):
                w = min(1024, cols - c0)
                i = cast_rr[0] % 2
                cast_rr[0] += 1
                bst, kb_ = ((JNK, "JNK"), (HB, "HB"))[i]
                S.dma("sync", XT[:, i, :w], src[r0:r0 + 128, c0:c0 + w], writes=["XT%d" % i])
                eng = ("vector", "gpsimd")[i]
                cp(bst[:, :w], XT[:, i, :w], ["XT%d" % i], [kb_], eng=eng)
                S.dma("sync", dst[r0:r0 + 128, c0:c0 + w], bst[:, :w], reads=[kb_],
                      writes=[dst.name if hasattr(dst, "name") else "w"])

    wkeys = {}
    for l in range(DEPTH):
        for nm, src, dst, rows, cols in (("win_ab", i_win_ab, wb_win_ab, D, NAB), ("win_g", i_win_g, wb_win_g, D, NG),
                                        ("wuq", i_wuq, wb_wuq, 384, 768), ("wuk", i_wuk, wb_wuk, 256, 512),
                                        ("wuv", i_wuv, wb_wuv, 256, 512), ("wa", i_wa, wb_wa, 512, D),
                                        ("wb", i_wb, wb_wb, 512, D), ("wout", i_wout, wb_wout, D, D)):
            cast_weight(src[l], dst[l], rows, cols)

    def norm_and_transpose(xsrc, T, par, tcol):
        xt = XT[:T, par, :]
        kx = "XT%d" % par
        S.dma("sync", xt, xsrc, writes=[kx])
        ms(SSQ[:T, par:par + 1], 0.0, ["SSQ%d" % par])
        act(JNK[:T, :], xt, AF.Square, [kx], ["JNK", "SSQ%d" % par], accum_out=SSQ[:T, par:par + 1])
        rsq(SSQ[:T, 2 + par:3 + par], SSQ[:T, par:par + 1], 0, ["SSQ%d" % par], ["SSR%d" % par], scale=1.0 / D)
        stt(HB[:T, :], xt, SSQ[:T, 2 + par:3 + par], NRW[:T, :], ALU.mult, ALU.mult, [kx, "SSR%d" % par, "NRW"], ["HB"])
        for kc in range(8):
            tr(PTB[:, kc * 128:kc * 128 + T], HB[:T, kc * 128:(kc + 1) * 128], IDB[:T, :T], ["HB", "IDB"], ["PTB"])
        cp(HT[:, par, :, :T], PTB[:].rearrange("p (k c) -> p k c", k=8)[:, :, :T], ["PTB"], ["HT%d" % par], eng="scalar")
        S.dma("sync", d_hT[:, :, tcol:tcol + T], HT[:, par, :, :T], reads=["HT%d" % par], writes=["d_hT%d" % tcol])

    def win_chunk(par, T, coff, M):
        ps, kp = psq()
        for kc in range(8):
            mm(ps[:M, :T], WIN[:, kc, coff:coff + M], HT[:, par, kc, :T], ["WAR", "HT%d" % par], [kp],
               start=(kc == 0), stop=(kc == 7))
        return ps, kp

    def shift_mix(ps, kp, ci, T, nseg, seglen, out, kout):
        m0 = vec(0, "mix0", ci); m1 = vec(0, "mix1", ci)
        act(PBF[:, 1:T + 1], ps[:, :T], AF.Copy, [kp, "VEC"], ["PBF"], scale=m1)
        if nseg == 1:
            ts(PBF[:, 0:1], CAR[:, ci, 0:1], m1, None, ALU.mult, None, ["CAR%d" % ci, "VEC"], ["PBF"])
        else:
            ts(PBF[:, 0:T:seglen], CAR[:, ci, 0:nseg], m1, None, ALU.mult, None, ["CAR%d" % ci, "VEC"], ["PBF"])
        stt(out, ps[:, :T], m0, PBF[:, 0:T], ALU.mult, ALU.add, [kp, "PBF", "VEC"], [kout])
        if nseg == 1:
            cp(CAR[:, ci, 0:1], ps[:, T - 1:T], [kp], ["CAR%d" % ci], eng="vector")
        else:
            cp(CAR[:, ci, 0:nseg], ps[:, seglen - 1:T:seglen], [kp], ["CAR%d" % ci], eng="vector")

    def rwkv_hp(hp, T, segs):
        R = RW
        XR, XK, XV, XG = R["XR"][:, :T], R["XK"][:, :T], R["XV"][:, :T], R["XG"][:, :T]
        SG, CS, AI, KKN, KMOD = R["SG"][:, :T], R["CS"][:, :T], R["AI"][:, :T], R["KKN"][:, :T], R["KMOD"][:, :T]
        EP, EM, EE = R["EP"][:, :T], R["EM"][:, :T], R["EE"][:, :T]
        AT, BT, KT, RT, BG, KG = R["AT"][:, :T], R["BT"][:, :T], R["KT"][:, :T], R["RT"][:, :T], R["BG"][:, :T], R["KG"][:, :T]
        T1, T2, BON = R["T1"][:, :T], R["T2"][:, :T], R["BON"][:, :T]
        nseg = len(segs)
        C = segs[0][1]
        if cfg.get("VERBOSE") and tuple(cur) == tuple(cfg.get("AT", ())):
            print("rr at", cur, "bank_rr", bank_rr[0], bank_rr[0] % 5, "pair_rr", pair_rr[0] % 2, "acc", acc_rr[0] % 2, "nops", len(S.ops))
        chk(410)
        pa, ka = psq()
        mm(pa[:, :T], W2A2[0:64, hp * 128:(hp + 1) * 128], TH[0:64, :T], ["W2A2", "TH"], [ka])
        act(SG, pa[:, :T], AF.Sigmoid, [ka, "VEC"], ["SG"], bias=vec(0, "w0", hp))
        chk(411)
        pb, kb = psq()
        mm(pb[:, :T], W2A2[64:128, hp * 128:(hp + 1) * 128], XW[64:128, :T], ["W2A2", "XW"], [kb])
        act(AI, pb[:, :T], AF.Sigmoid, [kb, "VEC"], ["AI"], bias=vec(0, "a0", hp))
        chk(412)
        for (c0, Cc, _) in segs:
            S.op("vector", lambda e, c0=c0, Cc=Cc: e.tensor_tensor_scan(
                R["CS"][:, c0:c0 + Cc], RMK[:, 0:Cc], R["SG"][:, c0:c0 + Cc], 0.0, ALU.mult, ALU.add),
                ["SG", "RMK"], ["CS"])
        chk(41)
        ts(T1, XK, vec(0, "kk", hp), None, ALU.mult, None, ["XK", "VEC"], ["T1"])
        act(T2, T1, AF.Square, ["T1"], ["T2"])
        pc, kc_ = psq()
        mm(pc[:, :T], OB1, T2, ["CM", "T2"], [kc_])
        rsq(T2, pc[:, :T], 2, [kc_], ["T2"])
        tt(KKN, T1, T2, ALU.mult, ["T1", "T2"], ["KKN"])
        ts(T1, AI, vec(0, "ka", hp), vec(0, "ka", hp), ALU.mult, ALU.subtract, ["AI", "VEC"], ["T1"])
        stt(KMOD, T1, 1.0, XK, ALU.add, ALU.mult, ["T1", "XK"], ["KMOD"])
        act(EP, CS, AF.Exp, ["CS"], ["EP"], scale=C0)
        act(EM, CS, AF.Exp, ["CS"], ["EM"], scale=-C0)
        tt(T1, CS, SG, ALU.subtract, ["CS", "SG"], ["T1"])
        act(T2, T1, AF.Exp, ["T1"], ["T2"], scale=-C0)
        for si, (c0, Cc, _) in enumerate(segs):
            ts(NB[:, si:si + 1], R["CS"][:, c0 + Cc - 1:c0 + Cc], -C0, None, ALU.mult, None, ["CS"], ["NB"])
        act(GC[:, 0:nseg], NB[:, 0:nseg], AF.Exp, ["NB"], ["GC"])
        for si, (c0, Cc, _) in enumerate(segs):
            act(R["EE"][:, c0:c0 + Cc], R["CS"][:, c0:c0 + Cc], AF.Exp, ["CS", "NB"], ["EE"], scale=C0,
                bias=NB[:, si:si + 1])
        stt(AT, KKN, -1.0, T2, ALU.mult, ALU.mult, ["KKN", "T2"], ["AT"])
        tt(T1, KKN, AI, ALU.mult, ["KKN", "AI"], ["T1"])
        tt(BT, T1, EP, ALU.mult, ["T1", "EP"], ["BT"])
        tt(BG, T1, EE, ALU.mult, ["T1", "EE"], ["BG"])
        tt(KT, KMOD, EP, ALU.mult, ["KMOD", "EP"], ["KT"])
        tt(KG, KMOD, EE, ALU.mult, ["KMOD", "EE"], ["KG"])
        tt(RT, XR, EM, ALU.mult, ["XR", "EM"], ["RT"])
        stt(T2, XR, vec(0, "rk", hp), KMOD, ALU.mult, ALU.mult, ["XR", "KMOD", "VEC"], ["T2"])
        pd, kd = psq()
        mm(pd[:, :T], OB1, T2, ["CM", "T2"], [kd])
        tt(BON, pd[:, :T], XV, ALU.mult, [kd, "XV"], ["BON"])
        chk(42)
        for si, (c0, Cc, slot) in enumerate(segs):
            sl = slice(c0, c0 + Cc)
            for nm, src, ksrc in (("A", R["AT"], "AT"), ("V", R["XV"], "XV"), ("BG", R["BG"], "BG"), ("KG", R["KG"], "KG")):
                pt_, kt_ = psq()
                tr(pt_[:Cc, :], src[:, sl], IDENT, [ksrc, "CM"], [kt_])
                cp(TM[nm][:Cc, :], pt_[:Cc, :], [kt_], ["TM_" + nm], eng=("scalar" if nm in ("A", "BG") else "vector"))
            chk(421)
            def pair_mm(lh, rh, klh, krh):
                ph, kph = psh()
                for h in range(2):
                    hs = slice(64 * h, 64 * h + 64)
                    mm(ph[:Cc, h, :Cc], lh[hs, sl], rh[hs, sl], [klh, krh], [kph])
                return ph, kph
            def mview(m):
                return m[:Cc, :].rearrange("p (h c) -> p h c", h=2)[:, :, :Cc]
            def tview(i):
                return TRI[:Cc, i, :].rearrange("p (h c) -> p h c", h=2)[:, :, :Cc]
            def pview(ph):
                return ph[:Cc, :, :Cc]
            for nm, lh, rh, klh, krh, ti in (("NT", R["BT"], R["AT"], "BT", "AT", 0), ("N", R["AT"], R["BT"], "AT", "BT", 1),
                                             ("AAK", R["KT"], R["AT"], "KT", "AT", 0), ("ARB", R["BT"], R["RT"], "BT", "RT", 2),
                                             ("ARK", R["KT"], R["RT"], "KT", "RT", 2)):
                ph, kph = pair_mm(lh, rh, klh, krh)
                tt(mview(MX[nm]), pview(ph), tview(ti), ALU.mult, [kph, "TRI"], ["MX_" + nm])
                chk(422)
            chk(43)
            ph, kph = psh()
            for h in range(2):
                mm(ph[:Cc, h, 64:128], MX["AAK"][:Cc, h * 128:h * 128 + Cc], TM["V"][:Cc, 64 * h:64 * h + 64],
                   ["MX_AAK", "TM_V"], [kph])
            xv0 = MX["X0"][:Cc, :].rearrange("p (h c) -> p h c", h=2)
            cp(xv0[:, :, 0:64], TM["A"][:Cc, :].rearrange("p (h c) -> p h c", h=2), ["TM_A"], ["MX_X0"])
            cp(xv0[:, :, 64:128], ph[:Cc, :, 64:128], [kph], ["MX_X0"], eng="scalar")
            nlev = int(math.ceil(math.log2(Cc)))
            Pn, Qn = "N", "NT"
            Xc, Xn = "X0", "X1"
            for lev in range(nlev):
                ph, kph = psh()
                for h in range(2):
                    mm(ph[:Cc, h, :], MX[Qn][:Cc, h * 128:h * 128 + Cc], MX[Xc][:Cc, h * 128:h * 128 + 128],
                       ["MX_" + Qn, "MX_" + Xc], [kph])
                tt(MX[Xn][:Cc, :].rearrange("p (h c) -> p h c", h=2), ph[:Cc, :, :],
                   MX[Xc][:Cc, :].rearrange("p (h c) -> p h c", h=2), ALU.add, [kph, "MX_" + Xc], ["MX_" + Xn])
                Xc, Xn = Xn, Xc
                if lev < nlev - 1:
                    P2 = "P0" if Pn in ("N", "P1") else "P1"
                    Q2 = "Q0" if Qn in ("NT", "Q1") else "Q1"
                    ph1, k1 = psh()
                    ph2, k2 = psh()
                    for h in range(2):
                        hs = slice(h * 128, h * 128 + Cc)
                        mm(ph1[:Cc, h, :Cc], MX[Qn][:Cc, hs], MX[Pn][:Cc, hs], ["MX_" + Qn, "MX_" + Pn], [k1])
                        mm(ph2[:Cc, h, :Cc], MX[Pn][:Cc, hs], MX[Qn][:Cc, hs], ["MX_" + Qn, "MX_" + Pn], [k2])
                    cp(mview(MX[P2]), pview(ph1), [k1], ["MX_" + P2], eng="scalar")
                    cp(mview(MX[Q2]), pview(ph2), [k2], ["MX_" + Q2])
                    Pn, Qn = P2, Q2
            chk(44)
            XF = MX[Xc]
            kXF = "MX_" + Xc
            prh, krh_ = psq()
            py0, ky0 = psq()
            pmt, kmt = psq()
            pnm, knm = psq()
            for h in range(2):
                hs = slice(64 * h, 64 * h + 64)
                ms_ = slice(h * 128, h * 128 + Cc)
                tp = (0, 64 * h)
                mm(prh[hs, :Cc], XF[:Cc, h * 128:h * 128 + 64], MX["ARB"][:Cc, ms_], [kXF, "MX_ARB"], [krh_], tile_position=tp)
                mm(py0[hs, :Cc], XF[:Cc, h * 128 + 64:h * 128 + 128], MX["ARB"][:Cc, ms_], [kXF, "MX_ARB"], [ky0],
                   start=True, stop=False, tile_position=tp)
                mm(py0[hs, :Cc], TM["V"][:Cc, hs], MX["ARK"][:Cc, ms_], ["TM_V", "MX_ARK"], [ky0],
                   start=False, stop=True, tile_position=tp)
                mm(pmt[hs, 0:64], XF[:Cc, h * 128:h * 128 + 64], TM["BG"][:Cc, hs], [kXF, "TM_BG"], [kmt], tile_position=tp)
                mm(pnm[hs, 0:64], TM["BG"][:Cc, hs], XF[:Cc, h * 128 + 64:h * 128 + 128], [kXF, "TM_BG"], [knm],
                   start=True, stop=False, tile_position=tp)
                mm(pnm[hs, 0:64], TM["KG"][:Cc, hs], TM["V"][:Cc, hs], ["TM_KG", "TM_V"], [knm],
                   start=False, stop=True, tile_position=tp)
            chk(45)
            tt(R["RH"][:, :Cc], prh[:, :Cc], R["RT"][:, sl], ALU.add, [krh_, "RT"], ["RH"])
            cp(R["Y0"][:, :Cc], py0[:, :Cc], [ky0], ["Y0"], eng="scalar")
            stt(MT[:, :], IDENT2[:, :], GC[:, si:si + 1], pmt[:, 0:64], ALU.mult, ALU.add, ["ID2", "GC", kmt], ["MT"])
            cp(NM[:, :], pnm[:, 0:64], [knm], ["NM"], eng="scalar")
            pyy, kyy = psh()
            pss, kss = psh()
            kst = "ST%d_%d" % (slot, hp)
            for h in range(2):
                hs = slice(64 * h, 64 * h + 64)
                tp = (64 * h, 64 * h)
                mm(pyy[hs, h, :Cc], ST[hs, slot, hp, :], R["RH"][hs, :Cc], [kst, "RH"], [kyy], tile_position=tp)
                mm(pss[hs, h, 0:64], MT[hs, :], ST[hs, slot, hp, :], [kst, "MT"], [kss], tile_position=tp)
            for h in range(2):
                hs = slice(64 * h, 64 * h + 64)
                tt(R["YY"][hs, sl], pyy[hs, h, :Cc], R["Y0"][hs, :Cc], ALU.add, [kyy, "Y0"], ["YY"])
                tt(ST[hs, slot, hp, :], pss[hs, h, 0:64], NM[hs, :], ALU.add, [kss, "NM"], [kst],
                   eng=("vector" if h == 0 else "gpsimd") if False else "vector")
        YY = R["YY"][:, :T]; YC = R["YC"][:, :T]; GS = R["GS"][:, :T]
        pm, kpm = psq()
        mm(pm[:, :T], OB64, YY, ["CM", "YY"], [kpm])
        tt(YC, YY, pm[:, :T], ALU.subtract, ["YY", kpm], ["YC"])
        act(T1, YC, AF.Square, ["YC"], ["T1"])
        pv, kpv = psq()
        mm(pv[:, :T], OB64, T1, ["CM", "T1"], [kpv])
        rsq(T2, pv[:, :T], 1, [kpv], ["T2"])
        tt(T1, YC, T2, ALU.mult, ["YC", "T2"], ["T1"])
        ts(T1, T1, vec(0, "lnw", hp), vec(0, "lnb", hp), ALU.mult, ALU.add, ["T1", "VEC"], ["T1"])
        tt(T1, T1, BON, ALU.add, ["T1", "BON"], ["T1"])
        act(GS, XG, AF.Silu, ["XG"], ["GS"])
        tt(OUTA[:, hp, :T], T1, GS, ALU.mult, ["T1", "GS"], ["OUTA"])

    IDENT2 = sb("IDENT2", [128, 64])
    S.dma("sync", IDENT2[0:64, :], i_cmat[0, 0:64, 0:64], writes=["ID2"])
    S.dma("sync", IDENT2[64:128, :], i_cmat[0, 64:128, 64:128], writes=["ID2"])

    def rms_feat(src, ksrc, nchunk, onesm, T, gain_nm, out32, kout32, outb=None, koutb=None):
        ps, kp = psq()
        for j in range(nchunk):
            act(T3[:, :T], src[:, j, :T], AF.Square, [ksrc], ["T3"])
            mm(ps[:, :T], onesm, T3[:, :T], ["CM", "T3"], [kp], start=(j == 0), stop=(j == nchunk - 1))
        rsq(T4[:, :T], ps[:, :T], 0, [kp], ["T4"])
        for j in range(nchunk):
            stt(out32[:, j, :T], src[:, j, :T], vec(0, gain_nm, j), T4[:, :T], ALU.mult, ALU.mult, [ksrc, "VEC", "T4"], [kout32])
        if outb is not None:
            cp(outb[:, :, :T], out32[:, :, :T], [kout32], [koutb], eng="gpsimd")

    def expand_kv(latb, klat, krb, kkr, T, slot, kcol, vtile):
        for h in range(8):
            ps, kp = psq()
            for j in range(2):
                mm(ps[0:64, :T], WUK[:, j, h * 64:(h + 1) * 64], latb[:, j, :T], ["WAR", klat], [kp], start=(j == 0), stop=(j == 1))
            cp(KH[:, :T], ps[0:64, :T], [kp], ["KH"], eng="scalar")
            act(T3[0:64, :T], ps[0:64, :T], AF.Square, [kp], ["T3"])
            p2, kp2 = psq()
            mm(p2[0:64, :T], OB64[0:64, 0:64], T3[0:64, :T], ["CM", "T3"], [kp2])
            rsq(T4[0:64, :T], p2[0:64, :T], 0, [kp2], ["T4"])
            stt(KTB[0:64, h, :T], KH[:, :T], vec(0, "knn", 0, 0, 64), T4[0:64, :T], ALU.mult, ALU.mult, ["KH", "VEC", "T4"], ["KTB"])
            cp(KTB[64:96, h, :T], krb[64:96, :T], [kkr], ["KTB"], eng="gpsimd")
        S.dma("sync", d_kS[slot, :, :, kcol:kcol + T].rearrange("h p c -> p h c"), KTB[:, :, :T], reads=["KTB"],
              writes=["d_kS%d" % slot])
        ps, kp = psb_full()
        for j in range(2):
            mm(ps[:T, :], latb[:, j, :T], WUV[:, j, :], ["WAR", klat], [kp], start=(j == 0), stop=(j == 1))
        cp(VTB[:T, :, 0:64], ps[:T, :].rearrange("p (h c) -> p h c", h=8), [kp], ["VTB"], eng="scalar")
        S.dma("sync", d_vS[slot, :, 0:T, vtile, :].rearrange("h p c -> p h c"), VTB[:T, :, :], reads=["VTB"],
              writes=["d_vS%d" % slot])

    def attention(T, slot, h, ktiles, kvpar, diag_tile):
        po, kpo = psq()
        n = len(ktiles)
        for i, (kcol, nk, vt) in enumerate(ktiles):
            ps, kp = psq()
            mm(ps[:nk, :T], KSB[:, kvpar, kcol:kcol + nk], QF[:, h, :T], ["KSB%d" % kvpar, "QF"], [kp])
            eb = EXB[:nk, i % 4, :T]
            ke = "EXB%d" % (i % 4)
            act(eb, ps[:nk, :T], AF.Exp, [kp], [ke], scale=96.0 ** -0.5)
            if i == diag_tile:
                tt(eb, eb, AMK[:nk, :T], ALU.mult, [ke, "AMK"], [ke], eng="gpsimd")
            mm(po[0:65, :T], VSB[:nk, kvpar, vt, :], eb, ["VSB%d" % kvpar, ke], [kpo], start=(i == 0), stop=(i == n - 1))
        cp(OSB[:, :T], po[0:65, :T], [kpo], ["OSB"], eng="scalar")
        pr, kpr = psq()
        mm(pr[0:64, :T], ONE65[64:65, :], OSB[64:65, :T], ["ONE65", "OSB"], [kpr])
        S.op("vector", lambda e: e.reciprocal(RCP[:, :T], pr[0:64, :T]), [kpr], ["RCP"])
        tt(OT[:, :T], OSB[0:64, :T], RCP[:, :T], ALU.mult, ["OSB", "RCP"], ["OT"])
        tt(YB[:, h, :T], OT[:, :T], SGB[:, h, :T], ALU.mult, ["OT", "SGB"], ["YB"])

    def phase_a_tile(l, par, T, tcol, xsrc, cst, segs, nseg, seglen, lat_out, kr_out, kv_slots):
        chk(1)
        norm_and_transpose(xsrc, T, par, tcol)
        chk(2)
        S.dma("sync", CST[:, :, :T], cst.rearrange("n p c -> p n c"), writes=["CST"])
        ps, kp = win_chunk(par, T, 0, 128)
        shift_mix(ps, kp, 0, T, nseg, seglen, XW[:, :T], "XW")
        act(TH[0:64, :T], XW[0:64, :T], AF.Tanh, ["XW"], ["TH"])
        chk(3)
        for hp in range(4):
            cur[2] = hp
            for j, nm in enumerate(("XR", "XK", "XV", "XG")):
                ci = 1 + 4 * hp + j
                ps, kp = win_chunk(par, T, ci * 128, 128)
                shift_mix(ps, kp, ci, T, nseg, seglen, RW[nm][:, :T], nm)
                chk(30 + ci)
            rwkv_hp(hp, T, segs)
            chk(4)
            chk(50 + hp)
        S.dma("sync", d_oa[:, :, tcol:tcol + T], OUTA[:, :, :T], reads=["OUTA"], writes=["d_oa%d" % tcol])
        for j in range(3):
            ps, kp = win_chunk(par, T, OFF_QC + j * 128, 128)
            cp(QC[:, j, :T], ps[:, :T], [kp], ["QC"], eng="scalar")
        rms_feat(QC, "QC", 3, O384, T, "qnorm", QC, "QC", QCN, "QCN")
        for j in range(2):
            ps, kp = win_chunk(par, T, OFF_CKV + j * 128, 128)
            cp(CKV[:, j, :T], ps[:, :T], [kp], ["CKV"], eng="scalar")
        rms_feat(CKV, "CKV", 2, O256, T, "kvnorm", LAT, "LAT", LATB, "LATB")
        for j in range(2):
            pt_, kt_ = psq()
            tr(pt_[:T, :], LAT[:, j, :T], IDENT, ["LAT", "CM"], [kt_])
            cp(LTM[:T, j * 128:(j + 1) * 128], pt_[:T, :], [kt_], ["LTM"], eng=("scalar" if j else "vector"))
        S.dma("sync", lat_out, LTM[:T, :], reads=["LTM"], writes=["o_lat"])
        ps, kp = win_chunk(par, T, OFF_KR, 96)
        cp(KR[64:96, :T], ps[64:96, :T], [kp], ["KR"], eng="scalar")
        act(T3[64:96, :T], ps[64:96, :T], AF.Square, [kp], ["T3"])
        p2, kp2 = psq()
        mm(p2[0:96, :T], OQ[64:96, 0:96], T3[64:96, :T], ["CM", "T3"], [kp2])
        rsq(T4[64:96, :T], p2[64:96, :T], 0, [kp2], ["T4"])
        stt(KRN[64:96, :T], KR[64:96, :T], vec(0, "knr", 0, 64, 96), T4[64:96, :T], ALU.mult, ALU.mult, ["KR", "VEC", "T4"], ["KRN"])
        p3, kp3 = psq()
        mm(p3[0:96, :T], ROT[64:96, 0:96], KRN[64:96, :T], ["CM", "KRN"], [kp3])
        tt(T3[64:96, :T], p3[64:96, :T], CST[64:96, 1, :T], ALU.mult, [kp3, "CST"], ["T3"])
        tt(KRF[64:96, :T], KRN[64:96, :T], CST[64:96, 0, :T], ALU.mult, ["KRN", "CST"], ["KRF"])
        tt(KRF[64:96, :T], KRF[64:96, :T], T3[64:96, :T], ALU.add, ["KRF", "T3"], ["KRF"])
        pt_, kt_ = psq()
        tr(pt_[:T, 0:96], KRF[0:96, :T], IDENT[0:96, 0:96], ["KRF", "CM"], [kt_])
        cp(KTM[:T, 64:96], pt_[:T, 64:96], [kt_], ["KTM"])
        S.dma("sync", kr_out, KTM[:T, 64:96], reads=["KTM"], writes=["o_kr"])
        for h in range(8):
            ps, kp = win_chunk(par, T, OFF_GB + h * 64, 64)
            act(SGB[:, h, :T], ps[0:64, :T], AF.Silu, [kp], ["SGB"])
        for h in range(8):
            ps, kp = psq()
            for j in range(3):
                mm(ps[0:96, :T], WUQ[:, j, h * 96:(h + 1) * 96], QCN[:, j, :T], ["WAR", "QCN"], [kp], start=(j == 0), stop=(j == 2))
            cp(QH[:, :T], ps[0:96, :T], [kp], ["QH"], eng="scalar")
            act(T3[0:96, :T], ps[0:96, :T], AF.Square, [kp], ["T3"])
            p2, kp2 = psq()
            mm(p2[0:96, :T], OQ[0:96, 0:96], T3[0:96, :T], ["CM", "T3"], [kp2])
            rsq(T4[0:96, :T], p2[0:96, :T], 0, [kp2], ["T4"])
            stt(QN[:, :T], QH[:, :T], vec(0, "qn", 0, 0, 96), T4[0:96, :T], ALU.mult, ALU.mult, ["QH", "VEC", "T4"], ["QN"])
            p3, kp3 = psq()
            mm(p3[0:96, :T], ROT[0:96, 0:96], QN[:, :T], ["CM", "QN"], [kp3])
            tt(T3[0:96, :T], p3[0:96, :T], CST[0:96, 1, :T], ALU.mult, [kp3, "CST"], ["T3"])
            tt(T4[0:96, :T], QN[:, :T], CST[0:96, 0, :T], ALU.mult, ["QN", "CST"], ["T4"])
            tt(QF[:, h, :T], T4[0:96, :T], T3[0:96, :T], ALU.add, ["T3", "T4"], ["QF"])
        chk(5)
        if len(kv_slots) == 1:
            _, _, slot, kcol, vtile = kv_slots[0]
            expand_kv(LATB, "LATB", KRF, "KRF", T, slot, kcol, vtile)
        else:
            expand_kv_multi(T, kv_slots)

    def expand_kv_multi(T, kv_slots):
        for h in range(8):
            ps, kp = psq()
            for j in range(2):
                mm(ps[0:64, :T], WUK[:, j, h * 64:(h + 1) * 64], LATB[:, j, :T], ["WAR", "LATB"], [kp], start=(j == 0), stop=(j == 1))
            cp(KH[:, :T], ps[0:64, :T], [kp], ["KH"], eng="scalar")
            act(T3[0:64, :T], ps[0:64, :T], AF.Square, [kp], ["T3"])
            p2, kp2 = psq()
            mm(p2[0:64, :T], OB64[0:64, 0:64], T3[0:64, :T], ["CM", "T3"], [kp2])
            rsq(T4[0:64, :T], p2[0:64, :T], 0, [kp2], ["T4"])
            stt(KTB[0:64, h, :T], KH[:, :T], vec(0, "knn", 0, 0, 64), T4[0:64, :T], ALU.mult, ALU.mult, ["KH", "VEC", "T4"], ["KTB"])
            cp(KTB[64:96, h, :T], KRF[64:96, :T], ["KRF"], ["KTB"], eng="gpsimd")
        for (c0, ncol, slot, kcol, vtile) in kv_slots:
            S.dma("sync", d_kS[slot, :, :, kcol:kcol + ncol].rearrange("h p c -> p h c"), KTB[:, :, c0:c0 + ncol],
                  reads=["KTB"], writes=["d_kS%d" % slot])
            ps, kp = psb_full()
            for j in range(2):
                mm(ps[:ncol, :], LATB[:, j, c0:c0 + ncol], WUV[:, j, :], ["WAR", "LATB"], [kp], start=(j == 0), stop=(j == 1))
            cp(VTB[:ncol, :, 0:64], ps[:ncol, :].rearrange("p (h c) -> p h c", h=8), [kp], ["VTB"], eng="scalar")
            S.dma("sync", d_vS[slot, :, 0:ncol, vtile, :].rearrange("h p c -> p h c"), VTB[:ncol, :, :], reads=["VTB"],
                  writes=["d_vS%d" % slot])

    kv_rr = [0]

    def attend_seq(T, qcol0, slot, nkcols, ktiles, diag_tile, tcol):
        for h in range(8):
            par = kv_rr[0] % 2
            kv_rr[0] += 1
            S.dma("sync", KSB[:, par, 0:nkcols], d_kS[slot, h, :, 0:nkcols], reads=["d_kS%d" % slot], writes=["KSB%d" % par])
            full = [vt for _, nk, vt in ktiles if nk == 128]
            part = [(nk, vt) for _, nk, vt in ktiles if nk != 128]
            if full:
                v0, v1 = min(full), max(full) + 1
                S.dma("sync", VSB[:, par, v0:v1, :], d_vS[slot, h, :, v0:v1, :], reads=["d_vS%d" % slot], writes=["VSB%d" % par])
            for nk, vt in part:
                S.dma("sync", VSB[:nk, par, vt, :], d_vS[slot, h, 0:nk, vt, :], reads=["d_vS%d" % slot], writes=["VSB%d" % par])
            attention_cols(T, qcol0, slot, h, ktiles, par, diag_tile)

    def attention_cols(T, q0, slot, h, ktiles, kvpar, diag_tile):
        po, kpo = psacc()
        n = len(ktiles)
        for i, (kcol, nk, vt) in enumerate(ktiles):
            ps, kp = psq()
            mm(ps[:nk, :T], KSB[:, kvpar, kcol:kcol + nk], QF[:, h, q0:q0 + T], ["KSB%d" % kvpar, "QF"], [kp])
            eb = EXB[:nk, i % 4, :T]
            ke = "EXB%d" % (i % 4)
            act(eb, ps[:nk, :T], AF.Exp, [kp], [ke], scale=96.0 ** -0.5)
            if i == diag_tile:
                tt(eb, eb, AMK[:nk, :T], ALU.mult, [ke, "AMK"], [ke], eng="gpsimd")
            mm(po[0:65, :T], VSB[:nk, kvpar, vt, :], eb, ["VSB%d" % kvpar, ke], [kpo], start=(i == 0), stop=(i == n - 1))
        cp(OSB[:, :T], po[0:65, :T], [kpo], ["OSB"], eng="scalar")
        pr, kpr = psq()
        mm(pr[0:64, :T], ONE65[64:65, :], OSB[64:65, :T], ["ONE65", "OSB"], [kpr])
        S.op("vector", lambda e: e.reciprocal(RCP[:, :T], pr[0:64, :T]), [kpr], ["RCP"])
        tt(OT[:, :T], OSB[0:64, :T], RCP[:, :T], ALU.mult, ["OSB", "RCP"], ["OT"])
        tt(YB[:, h, q0:q0 + T], OT[:, :T], SGB[:, h, q0:q0 + T], ALU.mult, ["OT", "SGB"], ["YB"])

    def phase_b_tile(l, T, tcol, xsrc, xdst):
        par = 0
        S.dma("sync", HT[:, par, :, :T], d_hT[:, :, tcol:tcol + T], reads=["d_hT%d" % tcol], writes=["HT%d" % par])
        S.dma("sync", OUTA[:, :, :T], d_oa[:, :, tcol:tcol + T], reads=["d_oa%d" % tcol], writes=["OUTA"])
        S.dma("sync", YB[:, :, :T], d_yb[:, :, tcol:tcol + T], reads=["d_yb%d" % tcol], writes=["YB"])
        S.dma("sync", XT[:T, 0, :], xsrc, reads=["d_xres%d" % tcol], writes=["XT0"])
        for j in range(8):
            pa_, ka_ = psq()
            for k in range(4):
                mm(pa_[:, :T], WA[:, k, j * 128:(j + 1) * 128], OUTA[:, k, :T], ["WAR", "OUTA"], [ka_], start=(k == 0), stop=(k == 3))
            pb_, kb_ = psq()
            for h in range(8):
                mm(pb_[:, :T], WB[0:64, h, j * 128:(j + 1) * 128], YB[:, h, :T], ["WAR", "YB"], [kb_], start=(h == 0), stop=(h == 7))
            for g in range(2):
                pg, kg = psq()
                for kc in range(8):
                    mm(pg[:, :T], WING[:, kc, (2 * j + g) * 128:(2 * j + g + 1) * 128], HT[:, par, kc, :T], ["WAR", "HT%d" % par], [kg],
                       start=(kc == 0), stop=(kc == 7))
                act(GAB[:, g, :T], pg[:, :T], AF.Sigmoid, [kg], ["GAB%d" % g])
            tt(T3[:, :T], pa_[:, :T], GAB[:, 0, :T], ALU.mult, [ka_, "GAB0"], ["T3"])
            tt(T4[:, :T], pb_[:, :T], GAB[:, 1, :T], ALU.mult, [kb_, "GAB1"], ["T4"])
            tt(MM[:, j, :T], T3[:, :T], T4[:, :T], ALU.add, ["T3", "T4"], ["MM"], eng="gpsimd")
        for c in range(2):
            po, kpo = psb_full()
            for kc in range(8):
                mm(po[:T, :], MM[:, kc, :T], WOUT[:, kc, c * 512:(c + 1) * 512], ["WAR", "MM"], [kpo], start=(kc == 0), stop=(kc == 7))
            tt(XT[:T, 1, c * 512:(c + 1) * 512], po[:T, :], XT[:T, 0, c * 512:(c + 1) * 512], ALU.add, [kpo, "XT0"], ["XT1"])
        for dst, kd in xdst:
            S.dma("sync", dst, XT[:T, 1, :], reads=["XT1"], writes=[kd])

    def load_w(dst, src, kchunks, key="WAR"):
        S.dma("sync", dst, src.rearrange("(k p) c -> p k c", p=128), reads=["wcast"], writes=[key])

    ptiles = [(NMETA, 0)] + [(128, NMETA + 128 * i) for i in range(NPT)]
    SCOL = L

    try:
      chk(0)
      for l in range(DEPTH):
          last = (l == DEPTH - 1)
          chk(100 + 10 * l)
          S.dma("sync", VEC[:], i_vecs[l], writes=["VEC"])
          S.dma("sync", NRW[:], i_normw[l], writes=["NRW"])
          S.dma("sync", W2A2[:], i_w2a2[l], writes=["W2A2"])
          S.dma("sync", WIN, wb_win_ab[l].rearrange("(k p) c -> p k c", p=128), reads=["b_win_ab"], writes=["WAR"])
          S.dma("sync", WUQ, wb_wuq[l].rearrange("(k p) c -> p k c", p=128), reads=["b_wuq"], writes=["WAR"])
          S.dma("sync", WUK, wb_wuk[l].rearrange("(k p) c -> p k c", p=128), reads=["b_wuk"], writes=["WAR"])
          S.dma("sync", WUV, wb_wuv[l].rearrange("(k p) c -> p k c", p=128), reads=["b_wuv"], writes=["WAR"])
          for ci in range(17):
              ms(CAR[:, ci, :], 0.0, ["CAR%d" % ci])
          for hp in range(4):
              ms(ST[:, 0, hp, :], 0.0, ["ST0_%d" % hp])
          for ti, (T, tcol) in enumerate(ptiles):
              if l == 0:
                  xsrc = i_meta if ti == 0 else i_xp[tcol - NMETA:tcol - NMETA + T, :]
              else:
                  xsrc = d_xres[tcol:tcol + T, :]
              par = ti % 2
              cur[0], cur[1] = l, ti
              chk(200 + 20 * l + ti)
              phase_a_tile(l, par, T, tcol, xsrc, i_csp[:, :, tcol:tcol + T], [(0, T, 0)], 1, T,
                           o_latp[l, tcol:tcol + T, :], o_krp[l, tcol:tcol + T, :], [(0, T, 0, tcol, ti)])
              if ti == 0:
                  ktiles = [(0, NMETA, 0)]
                  diag = -1
              else:
                  ktiles = [(0, NMETA, 0)] + [(NMETA + 128 * i, 128, 1 + i) for i in range(ti)]
                  diag = ti
              chk(6)
              attend_seq(T, 0, 0, tcol + T, ktiles, diag, tcol)
              chk(7)
              S.dma("sync", d_yb[:, :, tcol:tcol + T], YB[:, :, :T], reads=["YB"], writes=["d_yb%d" % tcol])
          chk(100 + 10 * l + 1)
          for hp in range(4):
              for h in range(2):
                  S.dma("sync", o_srp[l, 2 * hp + h].rearrange("v k -> k v"), ST[64 * h:64 * h + 64, 0, hp, :],
                        reads=["ST0_%d" % hp], writes=["o_srp"], slow=True)
          for ci in range(17):
              s0 = a_src(ci)
              S.dma("sync", o_shp[l, s0:s0 + 128].rearrange("(p o) -> p o", o=1), CAR[:, ci, 0:1], reads=["CAR%d" % ci],
                    writes=["o_shp"], slow=True)
          chk(100 + 10 * l + 2)
          for b in range(4):
              slot = 1 + b
              for t0 in range(0, PAST, 512):
                  for s in range(4):
                      S.dma("sync", PLT[:, s, :], i_clat[l, b, t0 + s * 128:t0 + (s + 1) * 128, :], writes=["PLT"])
                      S.dma("sync", PKT[:, s, 64:96], i_ckr[l, b, t0 + s * 128:t0 + (s + 1) * 128, :], writes=["PKT"])
                  for s in range(4):
                      for j in range(2):
                          pt_, kt_ = psq()
                          tr(pt_[:, :], PLT[:, s, j * 128:(j + 1) * 128], IDENT, ["PLT", "CM"], [kt_])
                          cp(LPB[:, j, s * 128:(s + 1) * 128], pt_[:, :], [kt_], ["LPB"], eng=("scalar" if j else "vector"))
                      pt_, kt_ = psq()
                      tr(pt_[0:96, :], PKT[:, s, :], IDENT, ["PKT", "CM"], [kt_])
                      cp(KPF[64:96, s * 128:(s + 1) * 128], pt_[64:96, :], [kt_], ["KPF"], eng="scalar")
                  for s in range(4):
                      expand_kv(LPB[:, :, s * 128:(s + 1) * 128], "LPB", KPF[:, s * 128:(s + 1) * 128], "KPF", 128, slot,
                                t0 + s * 128, (t0 // 128) + s)
          chk(100 + 10 * l + 3)
          for ci in range(17):
              s0 = a_src(ci)
              S.dma("sync", CAR[:, ci, 0:4], i_stsh[l, :, s0:s0 + 128].rearrange("b p -> p b"), writes=["CAR%d" % ci], slow=True)
          for b in range(4):
              for hp in range(4):
                  for h in range(2):
                      S.dma("sync", ST[64 * h:64 * h + 64, 1 + b, hp, :], i_strw[l, b, 2 * hp + h].rearrange("v k -> k v"),
                            writes=["ST%d_%d" % (1 + b, hp)], slow=True)
          xsrc = i_xs if l == 0 else d_xres[SCOL:SCOL + 64, :]
          phase_a_tile(l, 0, 64, SCOL, xsrc, i_css, [(16 * b, 16, 1 + b) for b in range(4)], 4, 16,
                       o_latn[l], o_krn[l], [(16 * b, 16, 1 + b, PAST, PAST // 128) for b in range(4)])
          for b in range(4):
              ktiles = [(128 * i, 128, i) for i in range(PAST // 128)] + [(PAST, 16, PAST // 128)]
              attend_seq(16, 16 * b, 1 + b, PAST + 16, ktiles, -1, SCOL)
          S.dma("sync", d_yb[:, :, SCOL:SCOL + 64], YB[:, :, :64], reads=["YB"], writes=["d_yb%d" % SCOL])
          for b in range(4):
              for hp in range(4):
                  for h in range(2):
                      S.dma("sync", o_srs[l, b, 2 * hp + h].rearrange("v k -> k v"), ST[64 * h:64 * h + 64, 1 + b, hp, :],
                            reads=["ST%d_%d" % (1 + b, hp)], writes=["o_srs"], slow=True)
          for ci in range(17):
              s0 = a_src(ci)
              S.dma("sync", o_shs[l, :, s0:s0 + 128].rearrange("b p -> p b"), CAR[:, ci, 0:4], reads=["CAR%d" % ci],
                    writes=["o_shs"], slow=True)
          chk(100 + 10 * l + 4)
          S.dma("sync", WING, wb_win_g[l].rearrange("(k p) c -> p k c", p=128), reads=["b_win_g"], writes=["WAR"])
          S.dma("sync", WA, wb_wa[l].rearrange("(k p) c -> p k c", p=128), reads=["b_wa"], writes=["WAR"])
          S.dma("sync", WB[0:64], wb_wb[l].rearrange("(k p) c -> p k c", p=64), reads=["b_wb"], writes=["WAR"])
          S.dma("sync", WOUT, wb_wout[l].rearrange("(k p) c -> p k c", p=128), reads=["b_wout"], writes=["WAR"])
          for ti, (T, tcol) in enumerate(ptiles):
              if l == 0:
                  xsrc = i_meta if ti == 0 else i_xp[tcol - NMETA:tcol - NMETA + T, :]
              else:
                  xsrc = d_xres[tcol:tcol + T, :]
              dsts = []
              if not last:
                  dsts.append((d_xres[tcol:tcol + T, :], "d_xres%d" % tcol))
              elif ti > 0:
                  dsts.append((o_yp[tcol - NMETA:tcol - NMETA + T, :], "o_yp"))
              phase_b_tile(l, T, tcol, xsrc, dsts)
          xsrc = i_xs if l == 0 else d_xres[SCOL:SCOL + 64, :]
          phase_b_tile(l, 64, SCOL, xsrc, [(o_ys, "o_ys")] if last else [(d_xres[SCOL:SCOL + 64, :], "d_xres%d" % SCOL)])

    except StopBuild:
        pass
    n_ops = S.emit(nc)
    if cfg.get("VERBOSE"):
        print("sched stats", {k: v[:2] for k, v in S.stats.items()}, S.stats["sync"][2], S.stats["sync"][3])
    es.close()
    return nc, n_ops


_CACHE = {}


def run(inputs, cfg):
    maps = host_layout(inputs, cfg)
    nc, n_ops = build(cfg)
    res = run_bass_kernel_spmd(nc, maps, core_ids=list(range(8)))
    return res.results


def kernel(**inputs):
    cfg = dict(DEPTH=4, SEQ=4096, PAST=2048)
    r = run(inputs, cfg)
    return assemble(r, cfg)


def assemble(r, cfg):
    DEPTH, SEQ, PAST = cfg["DEPTH"], cfg["SEQ"], cfg["PAST"]
    f = np.float32
    y_p = np.stack([np.asarray(r[c]["y_p"], f) for c in range(8)])
    y_s = np.concatenate([np.asarray(r[c]["y_s"], f).reshape(4, 16, D) for c in range(8)])
    srp = np.stack([np.asarray(r[c]["o_srp"], f) for c in range(8)], axis=1)
    shp = np.stack([np.asarray(r[c]["o_shp"], f) for c in range(8)], axis=1)[:, :, None, :]
    latp = np.stack([np.asarray(r[c]["o_latp"], f) for c in range(8)], axis=1)
    krp = np.stack([np.asarray(r[c]["o_krp"], f) for c in range(8)], axis=1)
    srs = np.concatenate([np.asarray(r[c]["o_srs"], f) for c in range(8)], axis=1)
    shs = np.concatenate([np.asarray(r[c]["o_shs"], f) for c in range(8)], axis=1)[:, :, None, :]
    latn = np.concatenate([np.asarray(r[c]["o_latn"], f).reshape(DEPTH, 4, 16, 256) for c in range(8)], axis=1)
    krn = np.concatenate([np.asarray(r[c]["o_krn"], f).reshape(DEPTH, 4, 16, 32) for c in range(8)], axis=1)
    return (y_p, y_s, srp, shp, latp, krp, srs, shs, latn, krn)
```

```python
import contextlib
import math
import numpy as np
import concourse.bass as bass
import concourse.mybir as mybir
from concourse.bass_utils import run_bass_kernel_spmd

F32 = mybir.dt.float32
BF16 = mybir.dt.bfloat16
AF = mybir.ActivationFunctionType
ALU = mybir.AluOpType
AX = mybir.AxisListType

ENGINES = ("tensor", "vector", "scalar", "gpsimd", "sync")


class StopBuild(Exception):
    pass


class Sched:
    def __init__(self, n_dma_sems=32):
        self.ops = []
        self.last_w = {}
        self.readers = {}
        self.n_dma_sems = n_dma_sems

    @staticmethod
    def _flat(keys):
        out = []
        for k in keys:
            if isinstance(k, tuple):
                out.extend(k)
            else:
                out.append(k)
        return out

    def _add(self, engine, fn, reads, writes, is_dma):
        reads = self._flat(reads)
        writes = self._flat(writes)
        idx = len(self.ops)
        deps = set()
        for k in reads:
            w = self.last_w.get(k)
            if w is not None:
                deps.add(w)
        for k in writes:
            w = self.last_w.get(k)
            if w is not None:
                deps.add(w)
            deps.update(self.readers.get(k, ()))
        for k in reads:
            self.readers.setdefault(k, []).append(idx)
        for k in writes:
            self.last_w[k] = idx
            self.readers[k] = []
        self.ops.append((engine, fn, deps, is_dma))
        return idx

    def op(self, engine, fn, reads=(), writes=()):
        return self._add(engine, fn, tuple(reads), tuple(writes), False)

    def dma(self, queue, out, in_, reads=(), writes=(), slow=False):
        if slow:
            fn = lambda eng: eng.dma_start(out=out, in_=in_, allow_slow_non_contiguous=True)
        else:
            fn = lambda eng: eng.dma_start(out=out, in_=in_)
        return self._add(queue, fn, tuple(reads), tuple(writes), True)

    def emit(self, nc, final_wait_engine="sync"):
        ops = self.ops
        n = len(ops)
        needed = [False] * n
        for o in ops:
            for d in o[2]:
                if o[0] == "tensor" and ops[d][0] == "tensor" and not ops[d][3]:
                    continue
                needed[d] = True
        eng_count = {e: 0 for e in ENGINES}
        dma_count = [0] * self.n_dma_sems
        token = [None] * n
        pre_wait = [None] * n
        rr = 0
        for i, o in enumerate(ops):
            if o[3]:
                s = rr % self.n_dma_sems
                rr += 1
                if dma_count[s] > 0:
                    pre_wait[i] = (("dma", s), dma_count[s] * 16)
                dma_count[s] += 1
                token[i] = (("dma", s), dma_count[s] * 16)
            elif needed[i]:
                e = o[0]
                eng_count[e] += 1
                token[i] = (("eng", e), eng_count[e])
        with contextlib.ExitStack() as es:
            sems = {}
            for e in ENGINES:
                sems[("eng", e)] = es.enter_context(nc.semaphore("s_" + e))
            for s in range(self.n_dma_sems):
                sems[("dma", s)] = es.enter_context(nc.semaphore("s_dma%d" % s))
            block = es.enter_context(nc.Block())
            streams = {e: [] for e in ENGINES}
            for i, o in enumerate(ops):
                streams[o[0]].append(i)
            final_tokens = {}
            for i, o in enumerate(ops):
                if o[3]:
                    k, v = token[i]
                    final_tokens[k] = max(final_tokens.get(k, 0), v)

            self.stats = {}

            def run(engname, eng):
                seen = {}
                nw = 0
                for i in streams[engname]:
                    o = ops[i]
                    waits = {}
                    if pre_wait[i] is not None:
                        k, v = pre_wait[i]
                        waits[k] = v
                    for d in o[2]:
                        if token[d] is None:
                            continue
                        k, v = token[d]
                        if engname == "tensor" and k == ("eng", "tensor"):
                            continue
                        if waits.get(k, 0) < v:
                            waits[k] = v
                    for k, v in waits.items():
                        if seen.get(k, 0) >= v:
                            continue
                        eng.wait_ge(sems[k], v)
                        seen[k] = v
                        nw += 1
                    ins = o[1](eng)
                    if token[i] is not None:
                        ins.then_inc(sems[token[i][0]], 16 if o[3] else 1)
                self.stats[engname] = (len(streams[engname]), nw, dict(eng_count), max(dma_count))
                if engname == final_wait_engine:
                    for k, v in final_tokens.items():
                        if seen.get(k, 0) < v:
                            eng.wait_ge(sems[k], v)

            block.tensor(lambda e: run("tensor", e))
            block.vector(lambda e: run("vector", e))
            block.scalar(lambda e: run("scalar", e))
            block.gpsimd(lambda e: run("gpsimd", e))
            block.sync(lambda e: run("sync", e))
        return n


D = 1024
NMETA = 16
C0 = math.exp(-0.5)
NORM_EPS = 1e-6
GN_EPS = 64e-5
NAB = 3424
NG = 2048
OFF_QC, OFF_CKV, OFF_KR, OFF_GB = 2176, 2560, 2816, 2912


def a_src(ci):
    if ci == 0:
        return 2048
    hp, j = divmod(ci - 1, 4)
    return j * 512 + hp * 128


VEC_NAMES = ([("mix0", c) for c in range(17)] + [("mix1", c) for c in range(17)] +
             [(nm, hp) for nm in ("w0", "a0", "kk", "ka", "rk", "lnw", "lnb") for hp in range(4)] +
             [("qnorm", j) for j in range(3)] + [("kvnorm", j) for j in range(2)] +
             [("qn", 0), ("knn", 0), ("knr", 0)])
VIDX = {k: i for i, k in enumerate(VEC_NAMES)}
NV = len(VEC_NAMES)


def host_layout(inp, cfg):
    DEPTH, SEQ, PAST = cfg["DEPTH"], cfg["SEQ"], cfg["PAST"]
    f = np.float32
    w_in = np.asarray(inp["w_in"], f)
    win_ab = np.zeros((DEPTH, D, NAB), f)
    for ci in range(17):
        s = a_src(ci)
        win_ab[:, :, ci * 128:(ci + 1) * 128] = w_in[:, :, s:s + 128]
    win_ab[:, :, OFF_QC:OFF_QC + 384] = w_in[:, :, 2176:2560]
    win_ab[:, :, OFF_CKV:OFF_CKV + 256] = w_in[:, :, 2560:2816]
    win_ab[:, :, OFF_KR + 64:OFF_KR + 96] = w_in[:, :, 2816:2848]
    win_ab[:, :, OFF_GB:OFF_GB + 512] = w_in[:, :, 2848:3360]
    win_g = np.zeros((DEPTH, D, NG), f)
    for j in range(8):
        win_g[:, :, (2 * j) * 128:(2 * j + 1) * 128] = w_in[:, :, 3360 + j * 128:3360 + (j + 1) * 128]
        win_g[:, :, (2 * j + 1) * 128:(2 * j + 2) * 128] = w_in[:, :, 4384 + j * 128:4384 + (j + 1) * 128]
    wukv = np.asarray(inp["mla_w_ukv"], f).reshape(DEPTH, 256, 8, 128)
    wuk = np.ascontiguousarray(wukv[:, :, :, :64]).reshape(DEPTH, 256, 512)
    wuv = np.ascontiguousarray(wukv[:, :, :, 64:]).reshape(DEPTH, 256, 512)
    w2a2 = np.concatenate([np.asarray(inp["rwkv_w2"], f), np.asarray(inp["rwkv_a2"], f)], axis=1)
    vecs = np.zeros((DEPTH, 128, NV), f)
    sm = np.asarray(inp["shift_mix"], f)
    for ci in range(17):
        s = a_src(ci)
        vecs[:, :, VIDX[("mix0", ci)]] = sm[:, 0, s:s + 128]
        vecs[:, :, VIDX[("mix1", ci)]] = sm[:, 1, s:s + 128]
    for nm, key in (("w0", "rwkv_w0"), ("a0", "rwkv_a0"), ("kk", "rwkv_k_k"), ("ka", "rwkv_k_a"),
                    ("rk", "rwkv_r_k"), ("lnw", "rwkv_ln_w"), ("lnb", "rwkv_ln_b")):
        v = np.asarray(inp[key], f).reshape(DEPTH, 512)
        for hp in range(4):
            vecs[:, :, VIDX[(nm, hp)]] = v[:, hp * 128:(hp + 1) * 128]
    for j in range(3):
        vecs[:, :, VIDX[("qnorm", j)]] = np.asarray(inp["mla_q_norm"], f)[:, j * 128:(j + 1) * 128]
    for j in range(2):
        vecs[:, :, VIDX[("kvnorm", j)]] = np.asarray(inp["mla_kv_norm"], f)[:, j * 128:(j + 1) * 128]
    vecs[:, 0:64, VIDX[("qn", 0)]] = np.asarray(inp["mla_qn_nope"], f)
    vecs[:, 64:96, VIDX[("qn", 0)]] = np.asarray(inp["mla_qn_rope"], f)
    vecs[:, 0:64, VIDX[("knn", 0)]] = np.asarray(inp["mla_kn_nope"], f)
    vecs[:, 64:96, VIDX[("knr", 0)]] = np.asarray(inp["mla_kn_rope"], f)
    normw = np.ascontiguousarray(np.broadcast_to(np.asarray(inp["norm_w"], f)[:, None, :], (DEPTH, 128, D)))

    ident = np.eye(128, dtype=f)
    ob1 = np.zeros((128, 128), f); ob1[:64, :64] = 1; ob1[64:, 64:] = 1
    oq = np.zeros((128, 128), f); oq[:64, :64] = 1 / 64; oq[64:96, 64:96] = 1 / 32
    rot = np.zeros((128, 128), f)
    for i in range(16):
        rot[64 + i + 16, 64 + i] = -1.0
        rot[64 + i, 64 + i + 16] = 1.0
    cmat = np.stack([ident, ob1, ob1 / 64.0, np.full((128, 128), 1 / 384, f), np.full((128, 128), 1 / 256, f),
                     oq, rot]).astype(f)
    pi = np.arange(128)[:, None]
    fi = np.arange(128)[None, :]
    tri = np.stack([(pi < fi), (pi > fi), (pi <= fi)]).astype(f)
    tri = np.ascontiguousarray(np.broadcast_to(tri[:, :, None, :], (3, 128, 2, 128))).reshape(3, 128, 256)
    amask = ((pi // 64) <= (fi // 64)).astype(f)
    inv = 10000.0 ** (-np.arange(0, 32, 2, dtype=np.float32) / 32)

    def cs_table(pos):
        ang = pos.astype(np.float32)[:, None] * inv[None, :]
        c = np.cos(ang).astype(f).T
        s = np.sin(ang).astype(f).T
        n = pos.shape[0]
        tab = np.zeros((2, 128, n), f)
        tab[0, :64] = 1.0
        tab[0, 64:80] = c; tab[0, 80:96] = c
        tab[1, 64:80] = s; tab[1, 80:96] = s
        return tab

    cs_p = cs_table(np.arange(-NMETA, SEQ))
    cs_s = cs_table(np.tile(PAST + np.arange(16), 4))
    common = dict(win_ab=win_ab, win_g=win_g, wuq=np.asarray(inp["mla_w_uq"], f), wuk=wuk, wuv=wuv,
                  wa=np.asarray(inp["w_branch_a"], f), wb=np.asarray(inp["w_branch_b"], f),
                  wout=np.asarray(inp["w_out"], f), w2a2=w2a2, vecs=vecs, normw=normw,
                  meta=np.asarray(inp["meta_tokens"], f), cmat=cmat, tri=tri, amask=amask,
                  cs_p=cs_p, cs_s=cs_s)
    maps = []
    for c in range(8):
        m = dict(common)
        m["xp"] = np.ascontiguousarray(np.asarray(inp["x_prompt"], f)[c])
        sl = slice(4 * c, 4 * c + 4)
        m["xs"] = np.ascontiguousarray(np.asarray(inp["x_sample"], f)[sl]).reshape(64, D)
        m["st_rwkv"] = np.ascontiguousarray(np.asarray(inp["state_rwkv"], f)[:, sl])
        m["st_shift"] = np.ascontiguousarray(np.asarray(inp["state_shift"], f)[:, sl, 0, :])
        m["c_lat"] = np.ascontiguousarray(np.asarray(inp["cache_mla_latent"], f)[:, sl])
        m["c_kr"] = np.ascontiguousarray(np.asarray(inp["cache_mla_krope"], f)[:, sl])
        maps.append(m)
    return maps


def build(cfg):
    DEPTH, SEQ, PAST = cfg["DEPTH"], cfg["SEQ"], cfg["PAST"]
    L = NMETA + SEQ
    NPT = SEQ // 128
    NKT_P = 1 + NPT
    NKT_S = PAST // 128 + 1
    LK = max(L, PAST + 16)
    NKT = max(NKT_P, NKT_S)
    nc = bass.Bass("TRN2", target_bir_lowering=False)
    S = Sched()
    es = contextlib.ExitStack()

    STOP = cfg.get("STOP", 99)

    AT = cfg.get("AT")
    cur = [0, 0, 0]

    def chk(level):
        if STOP == level and (AT is None or 100 <= level < 400 or tuple(cur[:len(AT)]) == tuple(AT)):
            raise StopBuild()

    def din(name, shape):
        return nc.dram_tensor(name, list(shape), F32, kind="ExternalInput").ap()

    def dout(name, shape):
        return nc.dram_tensor(name, list(shape), F32, kind="ExternalOutput").ap()

    def dscr(name, shape, dt):
        return nc.dram_tensor(name, list(shape), dt, kind="Internal").ap()

    i_win_ab = din("win_ab", [DEPTH, D, NAB]); i_win_g = din("win_g", [DEPTH, D, NG])
    i_wuq = din("wuq", [DEPTH, 384, 768]); i_wuk = din("wuk", [DEPTH, 256, 512]); i_wuv = din("wuv", [DEPTH, 256, 512])
    i_wa = din("wa", [DEPTH, 512, D]); i_wb = din("wb", [DEPTH, 512, D]); i_wout = din("wout", [DEPTH, D, D])
    i_w2a2 = din("w2a2", [DEPTH, 128, 512]); i_vecs = din("vecs", [DEPTH, 128, NV]); i_normw = din("normw", [DEPTH, 128, D])
    i_meta = din("meta", [NMETA, D]); i_cmat = din("cmat", [7, 128, 128]); i_tri = din("tri", [3, 128, 256])
    i_amask = din("amask", [128, 128]); i_csp = din("cs_p", [2, 128, L]); i_css = din("cs_s", [2, 128, 64])
    i_xp = din("xp", [SEQ, D]); i_xs = din("xs", [64, D])
    i_strw = din("st_rwkv", [DEPTH, 4, 8, 64, 64]); i_stsh = din("st_shift", [DEPTH, 4, 2176])
    i_clat = din("c_lat", [DEPTH, 4, PAST, 256]); i_ckr = din("c_kr", [DEPTH, 4, PAST, 32])

    o_yp = dout("y_p", [SEQ, D]); o_ys = dout("y_s", [64, D])
    o_srp = dout("o_srp", [DEPTH, 8, 64, 64]); o_shp = dout("o_shp", [DEPTH, 2176])
    o_latp = dout("o_latp", [DEPTH, L, 256]); o_krp = dout("o_krp", [DEPTH, L, 32])
    o_srs = dout("o_srs", [DEPTH, 4, 8, 64, 64]); o_shs = dout("o_shs", [DEPTH, 4, 2176])
    o_latn = dout("o_latn", [DEPTH, 64, 256]); o_krn = dout("o_krn", [DEPTH, 64, 32])

    d_xres = dscr("d_xres", [L + 64, D], F32)
    d_hT = dscr("d_hT", [128, 8, L + 64], BF16)
    d_oa = dscr("d_oa", [128, 4, L + 64], BF16)
    d_yb = dscr("d_yb", [64, 8, L + 64], BF16)
    d_kS = dscr("d_kS", [5, 8, 96, LK], BF16)
    d_vS = dscr("d_vS", [5, 8, 128, NKT, 65], BF16)
    wb_win_ab = dscr("b_win_ab", [DEPTH, D, NAB], BF16); wb_win_g = dscr("b_win_g", [DEPTH, D, NG], BF16)
    wb_wuq = dscr("b_wuq", [DEPTH, 384, 768], BF16); wb_wuk = dscr("b_wuk", [DEPTH, 256, 512], BF16)
    wb_wuv = dscr("b_wuv", [DEPTH, 256, 512], BF16)
    wb_wa = dscr("b_wa", [DEPTH, 512, D], BF16); wb_wb = dscr("b_wb", [DEPTH, 512, D], BF16)
    wb_wout = dscr("b_wout", [DEPTH, D, D], BF16)

    def sb(name, shape, dt=F32):
        return es.enter_context(nc.sbuf_tensor(name, list(shape), dt))

    WAR = sb("WAR", [128, 36864], BF16)
    off = 0
    def wview(n):
        nonlocal off
        v = WAR[:, off:off + n]
        off += n
        return v
    WIN = wview(8 * NAB).rearrange("p (k c) -> p k c", k=8)
    WUQ = wview(3 * 768).rearrange("p (k c) -> p k c", k=3)
    WUK = wview(2 * 512).rearrange("p (k c) -> p k c", k=2)
    WUV = wview(2 * 512).rearrange("p (k c) -> p k c", k=2)
    assert off <= 36864
    off = 0
    WING = wview(8 * NG).rearrange("p (k c) -> p k c", k=8)
    WA = wview(4 * D).rearrange("p (k c) -> p k c", k=4)
    WB = wview(8 * D).rearrange("p (k c) -> p k c", k=8)
    WOUT = wview(8 * D).rearrange("p (k c) -> p k c", k=8)
    assert off <= 36864

    CM = sb("CM", [128, 7, 128])
    IDENT, OB1, OB64, O384, O256, OQ, ROT = [CM[:, i, :] for i in range(7)]
    IDB = sb("IDB", [128, 128], BF16)
    TRI = sb("TRI", [128, 3, 256])
    AMK = sb("AMK", [128, 128], BF16)
    RMK = sb("RMK", [128, 128])
    ONE65 = sb("ONE65", [65, 64])
    W2A2 = sb("W2A2", [128, 512])
    VEC = sb("VEC", [128, NV])
    NRW = sb("NRW", [128, D])
    XT = sb("XT", [128, 2, D])
    SSQ = sb("SSQ", [128, 4])
    EPSV = sb("EPSV", [128, 4])
    JNK = sb("JNK", [128, D], BF16)
    HB = sb("HB", [128, D], BF16)
    HT = sb("HT", [128, 2, 8, 128], BF16)
    CST = sb("CST", [128, 2, 128])
    CAR = sb("CAR", [128, 17, 4])
    PBF = sb("PBF", [128, 132])
    XW = sb("XW", [128, 128]); TH = sb("TH", [128, 128])
    RW = {nm: sb("RW_" + nm, [128, 128]) for nm in
          ("XR", "XK", "XV", "XG", "SG", "CS", "AI", "KKN", "KMOD", "EP", "EM", "EE", "AT", "BT", "KT", "RT",
           "BG", "KG", "T1", "T2", "BON", "RH", "Y0", "YY", "YC", "GS")}
    NB = sb("NB", [128, 8]); GC = sb("GC", [128, 8])
    TM = {nm: sb("TM_" + nm, [128, 128]) for nm in ("A", "V", "BG", "KG")}
    MX = {nm: sb("MX_" + nm, [128, 256]) for nm in ("NT", "N", "AAK", "ARB", "ARK", "P0", "Q0", "P1", "Q1", "X0", "X1")}
    ST = sb("ST", [128, 5, 4, 64])
    MT = sb("MT", [128, 64]); NM = sb("NM", [128, 64])
    OUTA = sb("OUTA", [128, 4, 128], BF16)
    QC = sb("QC", [128, 3, 128]); QCN = sb("QCN", [128, 3, 128], BF16)
    CKV = sb("CKV", [128, 2, 128]); LAT = sb("LAT", [128, 2, 128]); LATB = sb("LATB", [128, 2, 128], BF16)
    LTM = sb("LTM", [128, 256])
    KR = sb("KR", [128, 128]); KRN = sb("KRN", [128, 128]); KRF = sb("KRF", [128, 128]); KTM = sb("KTM", [128, 96])
    SGB = sb("SGB", [64, 8, 128])
    QH = sb("QH", [96, 128]); QN = sb("QN", [96, 128]); QF = sb("QF", [96, 8, 128], BF16)
    KH = sb("KH", [64, 128]); KTB = sb("KTB", [96, 8, 128], BF16)
    VTB = sb("VTB", [128, 8, 65], BF16)
    KSB = sb("KSB", [96, 2, LK], BF16)
    VSB = sb("VSB", [128, 2, NKT, 65], BF16)
    EXB = sb("EXB", [128, 4, 128], BF16)
    OSB = sb("OSB", [65, 128]); RCP = sb("RCP", [64, 128]); OT = sb("OT", [64, 128])
    YB = sb("YB", [64, 8, 128], BF16)
    GAB = sb("GAB", [128, 2, 128])
    MM = sb("MM", [128, 8, 128], BF16)
    T3 = sb("T3", [128, 128]); T4 = sb("T4", [128, 128])
    T3P = sb("T3P", [128, 2, 128]); T4P = sb("T4P", [128, 2, 128])
    QHP = sb("QHP", [96, 2, 128]); QNP = sb("QNP", [96, 2, 128]); KHP = sb("KHP", [64, 2, 128])
    PLT = sb("PLT", [128, 4, 256])
    PKT = sb("PKT", [128, 4, 96])
    LPB = sb("LPB", [128, 2, 512], BF16)
    KPF = sb("KPF", [96, 512], BF16)

    PSALL = es.enter_context(nc.psum_tensor("psall", [128, 7 * 512], F32))
    PSB = [PSALL[:, i * 512:(i + 1) * 512] for i in range(7)]
    PSPAIR = PSALL[:, 4 * 512:6 * 512].rearrange("p (h c) -> p h c", h=2)
    PTB = es.enter_context(nc.psum_tensor("ptb", [128, 1024], BF16))
    bank_rr = [0]
    pair_rr = [0]
    acc_rr = [0]

    def psq():
        b = bank_rr[0] % 5
        bank_rr[0] += 1
        return PSB[b][:, 0:128], "B%d" % b

    def psb_full():
        b = bank_rr[0] % 5
        bank_rr[0] += 1
        return PSB[b][:, :], "B%d" % b

    def psh():
        i = pair_rr[0] % 2
        pair_rr[0] += 1
        v = PSALL[:, (2 * i) * 512:(2 * i + 2) * 512].rearrange("p (h c) -> p h c", h=2)[:, :, 0:128]
        return v, ("B%d" % (2 * i), "B%d" % (2 * i + 1))

    def psacc():
        b = 5 + acc_rr[0] % 2
        acc_rr[0] += 1
        return PSB[b][:, 0:128], "B%d" % b

    F32R = mybir.dt.float32r

    def mm(out, lhsT, rhs, r, w, start=True, stop=True, **kw):
        S.op("tensor", lambda e: e.matmul(out, lhsT, rhs, start=start, stop=stop, **kw), r, w)

    def tr(out, in_, ident, r, w):
        S.op("tensor", lambda e: e.transpose(out, in_, ident), r, w)

    def act(out, in_, func, r, w, **kw):
        S.op("scalar", lambda e: e.activation(out, in_, func, **kw), r, w)

    def tt(out, a, b, op, r, w, eng="vector"):
        S.op(eng, lambda e: e.tensor_tensor(out, a, b, op), r, w)

    def ts(out, a, s1, s2, op0, op1, r, w, eng="vector"):
        if op1 is None:
            S.op(eng, lambda e: e.tensor_scalar(out, a, s1, None, op0), r, w)
        else:
            S.op(eng, lambda e: e.tensor_scalar(out, a, s1, s2, op0, op1), r, w)

    def stt(out, a, s, b, op0, op1, r, w, eng="vector"):
        S.op(eng, lambda e: e.scalar_tensor_tensor(out, a, s, b, op0, op1), r, w)

    def rsq(out, in_, epsi, r, w, scale=1.0):
        np_ = out.shape[0]
        bp = out.base_partition()
        S.op("scalar", lambda e: e.activation(out, in_, AF.Sqrt, bias=EPSV[bp:bp + np_, epsi:epsi + 1], scale=scale),
             list(r) + ["EPSV"], w)
        S.op("vector", lambda e: e.reciprocal(out, out), w, w)

    def cp(out, in_, r, w, eng="vector"):
        if eng == "scalar":
            S.op("scalar", lambda e: e.copy(out, in_), r, w)
        else:
            S.op(eng, lambda e: e.tensor_copy(out, in_), r, w)

    def ms(ap, val, w, eng="vector"):
        S.op(eng, lambda e: e.memset(ap, val), (), w)

    def vec(l_unused, nm, i=0, lo=0, hi=128):
        c = VIDX[(nm, i)]
        return VEC[lo:hi, c:c + 1]

    S.dma("sync", CM[:], i_cmat.rearrange("n p c -> p n c"), writes=["CM"])
    S.dma("sync", TRI[:], i_tri.rearrange("n p c -> p n c"), writes=["TRI"])
    S.dma("sync", T3[:], i_amask, writes=["T3"])
    cp(AMK[:], T3[:], ["T3"], ["AMK"])
    cp(IDB[:], IDENT, ["CM"], ["IDB"])
    ms(RMK[:], 1.0, ["RMK"])
    ms(RMK[:, 0:1], 0.0, ["RMK"])
    ms(ONE65[:], 1.0, ["ONE65"])
    ms(VSB[:], 1.0, ["VSB0", "VSB1"])
    ms(VTB[:], 1.0, ["VTB"])
    ms(PKT[:], 0.0, ["PKT"])
    ms(KRF[:], 0.0, ["KRF"])
    ms(SSQ[:], 0.0, ["SSQ"])
    ms(EPSV[:, 0:1], NORM_EPS, ["EPSV"])
    ms(EPSV[:, 1:2], GN_EPS, ["EPSV"])
    ms(EPSV[:, 2:3], 1e-24, ["EPSV"])

    cast_rr = [0]

    def cast_weight(src, dst, rows, cols):
        for r0 in range(0, rows, 128):
            for c0 in range(0, cols, 1024):
                w = min(1024, cols - c0)
                i = cast_rr[0] % 2
                cast_rr[0] += 1
                bst, kb_ = ((JNK, "JNK"), (HB, "HB"))[i]
                S.dma("sync", XT[:, i, :w], src[r0:r0 + 128, c0:c0 + w], writes=["XT%d" % i])
                eng = ("vector", "gpsimd")[i]
                cp(bst[:, :w], XT[:, i, :w], ["XT%d" % i], [kb_], eng=eng)
                S.dma("sync", dst[r0:r0 + 128, c0:c0 + w], bst[:, :w], reads=[kb_],
                      writes=[dst.name if hasattr(dst, "name") else "w"])

    wkeys = {}
    for l in range(DEPTH):
        for nm, src, dst, rows, cols in (("win_ab", i_win_ab, wb_win_ab, D, NAB), ("win_g", i_win_g, wb_win_g, D, NG),
                                        ("wuq", i_wuq, wb_wuq, 384, 768), ("wuk", i_wuk, wb_wuk, 256, 512),
                                        ("wuv", i_wuv, wb_wuv, 256, 512), ("wa", i_wa, wb_wa, 512, D),
                                        ("wb", i_wb, wb_wb, 512, D), ("wout", i_wout, wb_wout, D, D)):
            cast_weight(src[l], dst[l], rows, cols)

    def norm_and_transpose(xsrc, T, par, tcol):
        xt = XT[:T, par, :]
        kx = "XT%d" % par
        S.dma("sync", xt, xsrc, writes=[kx])
        ms(SSQ[:T, par:par + 1], 0.0, ["SSQ%d" % par])
        act(JNK[:T, :], xt, AF.Square, [kx], ["JNK", "SSQ%d" % par], accum_out=SSQ[:T, par:par + 1])
        rsq(SSQ[:T, 2 + par:3 + par], SSQ[:T, par:par + 1], 0, ["SSQ%d" % par], ["SSR%d" % par], scale=1.0 / D)
        stt(HB[:T, :], xt, SSQ[:T, 2 + par:3 + par], NRW[:T, :], ALU.mult, ALU.mult, [kx, "SSR%d" % par, "NRW"], ["HB"])
        for kc in range(8):
            tr(PTB[:, kc * 128:kc * 128 + T], HB[:T, kc * 128:(kc + 1) * 128], IDB[:T, :T], ["HB", "IDB"], ["PTB"])
        cp(HT[:, par, :, :T], PTB[:].rearrange("p (k c) -> p k c", k=8)[:, :, :T], ["PTB"], ["HT%d" % par], eng="scalar")
        S.dma("sync", d_hT[:, :, tcol:tcol + T], HT[:, par, :, :T], reads=["HT%d" % par], writes=["d_hT%d" % tcol])

    def win_chunk(par, T, coff, M):
        ps, kp = psq()
        for kc in range(8):
            mm(ps[:M, :T], WIN[:, kc, coff:coff + M], HT[:, par, kc, :T], ["WAR", "HT%d" % par], [kp],
               start=(kc == 0), stop=(kc == 7))
        return ps, kp

    def shift_mix(ps, kp, ci, T, nseg, seglen, out, kout):
        m0 = vec(0, "mix0", ci); m1 = vec(0, "mix1", ci)
        act(PBF[:, 1:T + 1], ps[:, :T], AF.Copy, [kp, "VEC"], ["PBF"], scale=m1)
        if nseg == 1:
            ts(PBF[:, 0:1], CAR[:, ci, 0:1], m1, None, ALU.mult, None, ["CAR%d" % ci, "VEC"], ["PBF"])
        else:
            ts(PBF[:, 0:T:seglen], CAR[:, ci, 0:nseg], m1, None, ALU.mult, None, ["CAR%d" % ci, "VEC"], ["PBF"])
        stt(out, ps[:, :T], m0, PBF[:, 0:T], ALU.mult, ALU.add, [kp, "PBF", "VEC"], [kout])
        if nseg == 1:
            cp(CAR[:, ci, 0:1], ps[:, T - 1:T], [kp], ["CAR%d" % ci], eng="vector")
        else:
            cp(CAR[:, ci, 0:nseg], ps[:, seglen - 1:T:seglen], [kp], ["CAR%d" % ci], eng="vector")

    def rwkv_hp(hp, T, segs):
        R = RW
        XR, XK, XV, XG = R["XR"][:, :T], R["XK"][:, :T], R["XV"][:, :T], R["XG"][:, :T]
        SG, CS, AI, KKN, KMOD = R["SG"][:, :T], R["CS"][:, :T], R["AI"][:, :T], R["KKN"][:, :T], R["KMOD"][:, :T]
        EP, EM, EE = R["EP"][:, :T], R["EM"][:, :T], R["EE"][:, :T]
        AT, BT, KT, RT, BG, KG = R["AT"][:, :T], R["BT"][:, :T], R["KT"][:, :T], R["RT"][:, :T], R["BG"][:, :T], R["KG"][:, :T]
        T1, T2, BON = R["T1"][:, :T], R["T2"][:, :T], R["BON"][:, :T]
        nseg = len(segs)
        C = segs[0][1]
        if cfg.get("VERBOSE") and tuple(cur) == tuple(cfg.get("AT", ())):
            print("rr at", cur, "bank_rr", bank_rr[0], bank_rr[0] % 5, "pair_rr", pair_rr[0] % 2, "acc", acc_rr[0] % 2, "nops", len(S.ops))
        chk(410)
        pa, ka = psq()
        mm(pa[:, :T], W2A2[0:64, hp * 128:(hp + 1) * 128], TH[0:64, :T], ["W2A2", "TH"], [ka])
        act(SG, pa[:, :T], AF.Sigmoid, [ka, "VEC"], ["SG"], bias=vec(0, "w0", hp))
        chk(411)
        pb, kb = psq()
        mm(pb[:, :T], W2A2[64:128, hp * 128:(hp + 1) * 128], XW[64:128, :T], ["W2A2", "XW"], [kb])
        act(AI, pb[:, :T], AF.Sigmoid, [kb, "VEC"], ["AI"], bias=vec(0, "a0", hp))
        chk(412)
        for (c0, Cc, _) in segs:
            S.op("vector", lambda e, c0=c0, Cc=Cc: e.tensor_tensor_scan(
                R["CS"][:, c0:c0 + Cc], RMK[:, 0:Cc], R["SG"][:, c0:c0 + Cc], 0.0, ALU.mult, ALU.add),
                ["SG", "RMK"], ["CS"])
        chk(41)
        ts(T1, XK, vec(0, "kk", hp), None, ALU.mult, None, ["XK", "VEC"], ["T1"])
        act(T2, T1, AF.Square, ["T1"], ["T2"])
        pc, kc_ = psq()
        mm(pc[:, :T], OB1, T2, ["CM", "T2"], [kc_])
        rsq(T2, pc[:, :T], 2, [kc_], ["T2"])
        tt(KKN, T1, T2, ALU.mult, ["T1", "T2"], ["KKN"])
        ts(T1, AI, vec(0, "ka", hp), vec(0, "ka", hp), ALU.mult, ALU.subtract, ["AI", "VEC"], ["T1"])
        stt(KMOD, T1, 1.0, XK, ALU.add, ALU.mult, ["T1", "XK"], ["KMOD"])
        act(EP, CS, AF.Exp, ["CS"], ["EP"], scale=C0)
        act(EM, CS, AF.Exp, ["CS"], ["EM"], scale=-C0)
        tt(T1, CS, SG, ALU.subtract, ["CS", "SG"], ["T1"])
        act(T2, T1, AF.Exp, ["T1"], ["T2"], scale=-C0)
        for si, (c0, Cc, _) in enumerate(segs):
            ts(NB[:, si:si + 1], R["CS"][:, c0 + Cc - 1:c0 + Cc], -C0, None, ALU.mult, None, ["CS"], ["NB"])
        act(GC[:, 0:nseg], NB[:, 0:nseg], AF.Exp, ["NB"], ["GC"])
        for si, (c0, Cc, _) in enumerate(segs):
            act(R["EE"][:, c0:c0 + Cc], R["CS"][:, c0:c0 + Cc], AF.Exp, ["CS", "NB"], ["EE"], scale=C0,
                bias=NB[:, si:si + 1])
        stt(AT, KKN, -1.0, T2, ALU.mult, ALU.mult, ["KKN", "T2"], ["AT"])
        tt(T1, KKN, AI, ALU.mult, ["KKN", "AI"], ["T1"])
        tt(BT, T1, EP, ALU.mult, ["T1", "EP"], ["BT"])
        tt(BG, T1, EE, ALU.mult, ["T1", "EE"], ["BG"])
        tt(KT, KMOD, EP, ALU.mult, ["KMOD", "EP"], ["KT"])
        tt(KG, KMOD, EE, ALU.mult, ["KMOD", "EE"], ["KG"])
        tt(RT, XR, EM, ALU.mult, ["XR", "EM"], ["RT"])
        stt(T2, XR, vec(0, "rk", hp), KMOD, ALU.mult, ALU.mult, ["XR", "KMOD", "VEC"], ["T2"])
        pd, kd = psq()
        mm(pd[:, :T], OB1, T2, ["CM", "T2"], [kd])
        tt(BON, pd[:, :T], XV, ALU.mult, [kd, "XV"], ["BON"])
        chk(42)
        for si, (c0, Cc, slot) in enumerate(segs):
            sl = slice(c0, c0 + Cc)
            for nm, src, ksrc in (("A", R["AT"], "AT"), ("V", R["XV"], "XV"), ("BG", R["BG"], "BG"), ("KG", R["KG"], "KG")):
                pt_, kt_ = psq()
                tr(pt_[:Cc, :], src[:, sl], IDENT, [ksrc, "CM"], [kt_])
                cp(TM[nm][:Cc, :], pt_[:Cc, :], [kt_], ["TM_" + nm], eng=("scalar" if nm in ("A", "BG") else "vector"))
            chk(421)
            def pair_mm(lh, rh, klh, krh):
                ph, kph = psh()
                for h in range(2):
                    hs = slice(64 * h, 64 * h + 64)
                    mm(ph[:Cc, h, :Cc], lh[hs, sl], rh[hs, sl], [klh, krh], [kph])
                return ph, kph
            def mview(m):
                return m[:Cc, :].rearrange("p (h c) -> p h c", h=2)[:, :, :Cc]
            def tview(i):
                return TRI[:Cc, i, :].rearrange("p (h c) -> p h c", h=2)[:, :, :Cc]
            def pview(ph):
                return ph[:Cc, :, :Cc]
            for nm, lh, rh, klh, krh, ti in (("NT", R["BT"], R["AT"], "BT", "AT", 0), ("N", R["AT"], R["BT"], "AT", "BT", 1),
                                             ("AAK", R["KT"], R["AT"], "KT", "AT", 0), ("ARB", R["BT"], R["RT"], "BT", "RT", 2),
                                             ("ARK", R["KT"], R["RT"], "KT", "RT", 2)):
                ph, kph = pair_mm(lh, rh, klh, krh)
                tt(mview(MX[nm]), pview(ph), tview(ti), ALU.mult, [kph, "TRI"], ["MX_" + nm])
                chk(422)
            chk(43)
            ph, kph = psh()
            for h in range(2):
                mm(ph[:Cc, h, 64:128], MX["AAK"][:Cc, h * 128:h * 128 + Cc], TM["V"][:Cc, 64 * h:64 * h + 64],
                   ["MX_AAK", "TM_V"], [kph])
            xv0 = MX["X0"][:Cc, :].rearrange("p (h c) -> p h c", h=2)
            cp(xv0[:, :, 0:64], TM["A"][:Cc, :].rearrange("p (h c) -> p h c", h=2), ["TM_A"], ["MX_X0"])
            cp(xv0[:, :, 64:128], ph[:Cc, :, 64:128], [kph], ["MX_X0"], eng="scalar")
            nlev = int(math.ceil(math.log2(Cc)))
            Pn, Qn = "N", "NT"
            Xc, Xn = "X0", "X1"
            for lev in range(nlev):
                ph, kph = psh()
                for h in range(2):
                    mm(ph[:Cc, h, :], MX[Qn][:Cc, h * 128:h * 128 + Cc], MX[Xc][:Cc, h * 128:h * 128 + 128],
                       ["MX_" + Qn, "MX_" + Xc], [kph])
                tt(MX[Xn][:Cc, :].rearrange("p (h c) -> p h c", h=2), ph[:Cc, :, :],
                   MX[Xc][:Cc, :].rearrange("p (h c) -> p h c", h=2), ALU.add, [kph, "MX_" + Xc], ["MX_" + Xn])
                Xc, Xn = Xn, Xc
                if lev < nlev - 1:
                    P2 = "P0" if Pn in ("N", "P1") else "P1"
                    Q2 = "Q0" if Qn in ("NT", "Q1") else "Q1"
                    ph1, k1 = psh()
                    ph2, k2 = psh()
                    for h in range(2):
                        hs = slice(h * 128, h * 128 + Cc)
                        mm(ph1[:Cc, h, :Cc], MX[Qn][:Cc, hs], MX[Pn][:Cc, hs], ["MX_" + Qn, "MX_" + Pn], [k1])
                        mm(ph2[:Cc, h, :Cc], MX[Pn][:Cc, hs], MX[Qn][:Cc, hs], ["MX_" + Qn, "MX_" + Pn], [k2])
                    cp(mview(MX[P2]), pview(ph1), [k1], ["MX_" + P2], eng="scalar")
                    cp(mview(MX[Q2]), pview(ph2), [k2], ["MX_" + Q2])
                    Pn, Qn = P2, Q2
            chk(44)
            XF = MX[Xc]
            kXF = "MX_" + Xc
            prh, krh_ = psq()
            py0, ky0 = psq()
            pmt, kmt = psq()
            pnm, knm = psq()
            for h in range(2):
                hs = slice(64 * h, 64 * h + 64)
                ms_ = slice(h * 128, h * 128 + Cc)
                tp = (0, 64 * h)
                mm(prh[hs, :Cc], XF[:Cc, h * 128:h * 128 + 64], MX["ARB"][:Cc, ms_], [kXF, "MX_ARB"], [krh_], tile_position=tp)
                mm(py0[hs, :Cc], XF[:Cc, h * 128 + 64:h * 128 + 128], MX["ARB"][:Cc, ms_], [kXF, "MX_ARB"], [ky0],
                   start=True, stop=False, tile_position=tp)
                mm(py0[hs, :Cc], TM["V"][:Cc, hs], MX["ARK"][:Cc, ms_], ["TM_V", "MX_ARK"], [ky0],
                   start=False, stop=True, tile_position=tp)
                mm(pmt[hs, 0:64], XF[:Cc, h * 128:h * 128 + 64], TM["BG"][:Cc, hs], [kXF, "TM_BG"], [kmt], tile_position=tp)
                mm(pnm[hs, 0:64], TM["BG"][:Cc, hs], XF[:Cc, h * 128 + 64:h * 128 + 128], [kXF, "TM_BG"], [knm],
                   start=True, stop=False, tile_position=tp)
                mm(pnm[hs, 0:64], TM["KG"][:Cc, hs], TM["V"][:Cc, hs], ["TM_KG", "TM_V"], [knm],
                   start=False, stop=True, tile_position=tp)
            chk(45)
            tt(R["RH"][:, :Cc], prh[:, :Cc], R["RT"][:, sl], ALU.add, [krh_, "RT"], ["RH"])
            cp(R["Y0"][:, :Cc], py0[:, :Cc], [ky0], ["Y0"], eng="scalar")
            stt(MT[:, :], IDENT2[:, :], GC[:, si:si + 1], pmt[:, 0:64], ALU.mult, ALU.add, ["ID2", "GC", kmt], ["MT"])
            cp(NM[:, :], pnm[:, 0:64], [knm], ["NM"], eng="scalar")
            pyy, kyy = psh()
            pss, kss = psh()
            kst = "ST%d_%d" % (slot, hp)
            for h in range(2):
                hs = slice(64 * h, 64 * h + 64)
                tp = (64 * h, 64 * h)
                mm(pyy[hs, h, :Cc], ST[hs, slot, hp, :], R["RH"][hs, :Cc], [kst, "RH"], [kyy], tile_position=tp)
                mm(pss[hs, h, 0:64], MT[hs, :], ST[hs, slot, hp, :], [kst, "MT"], [kss], tile_position=tp)
            for h in range(2):
                hs = slice(64 * h, 64 * h + 64)
                tt(R["YY"][hs, sl], pyy[hs, h, :Cc], R["Y0"][hs, :Cc], ALU.add, [kyy, "Y0"], ["YY"])
                tt(ST[hs, slot, hp, :], pss[hs, h, 0:64], NM[hs, :], ALU.add, [kss, "NM"], [kst],
                   eng=("vector" if h == 0 else "gpsimd") if False else "vector")
        YY = R["YY"][:, :T]; YC = R["YC"][:, :T]; GS = R["GS"][:, :T]
        pm, kpm = psq()
        mm(pm[:, :T], OB64, YY, ["CM", "YY"], [kpm])
        tt(YC, YY, pm[:, :T], ALU.subtract, ["YY", kpm], ["YC"])
        act(T1, YC, AF.Square, ["YC"], ["T1"])
        pv, kpv = psq()
        mm(pv[:, :T], OB64, T1, ["CM", "T1"], [kpv])
        rsq(T2, pv[:, :T], 1, [kpv], ["T2"])
        tt(T1, YC, T2, ALU.mult, ["YC", "T2"], ["T1"])
        ts(T1, T1, vec(0, "lnw", hp), vec(0, "lnb", hp), ALU.mult, ALU.add, ["T1", "VEC"], ["T1"])
        tt(T1, T1, BON, ALU.add, ["T1", "BON"], ["T1"])
        act(GS, XG, AF.Silu, ["XG"], ["GS"])
        tt(OUTA[:, hp, :T], T1, GS, ALU.mult, ["T1", "GS"], ["OUTA"])

    IDENT2 = sb("IDENT2", [128, 64])
    S.dma("sync", IDENT2[0:64, :], i_cmat[0, 0:64, 0:64], writes=["ID2"])
    S.dma("sync", IDENT2[64:128, :], i_cmat[0, 64:128, 64:128], writes=["ID2"])

    def rms_feat(src, ksrc, nchunk, onesm, T, gain_nm, out32, kout32, outb=None, koutb=None):
        ps, kp = psq()
        for j in range(nchunk):
            act(T3[:, :T], src[:, j, :T], AF.Square, [ksrc], ["T3"])
            mm(ps[:, :T], onesm, T3[:, :T], ["CM", "T3"], [kp], start=(j == 0), stop=(j == nchunk - 1))
        rsq(T4[:, :T], ps[:, :T], 0, [kp], ["T4"])
        for j in range(nchunk):
            stt(out32[:, j, :T], src[:, j, :T], vec(0, gain_nm, j), T4[:, :T], ALU.mult, ALU.mult, [ksrc, "VEC", "T4"], [kout32])
        if outb is not None:
            cp(outb[:, :, :T], out32[:, :, :T], [kout32], [koutb], eng="gpsimd")

    def expand_kv(latb, klat, krb, kkr, T, slot, kcol, vtile):
        for h in range(8):
            ps, kp = psq()
            for j in range(2):
                mm(ps[0:64, :T], WUK[:, j, h * 64:(h + 1) * 64], latb[:, j, :T], ["WAR", klat], [kp], start=(j == 0), stop=(j == 1))
            q = h % 2
            k3, k4, kkh = "T3P%d" % q, "T4P%d" % q, "KHP%d" % q
            cp(KHP[:, q, :T], ps[0:64, :T], [kp], [kkh], eng="scalar")
            act(T3P[0:64, q, :T], ps[0:64, :T], AF.Square, [kp], [k3])
            p2, kp2 = psq()
            mm(p2[0:64, :T], OB64[0:64, 0:64], T3P[0:64, q, :T], ["CM", k3], [kp2])
            rsq(T4P[0:64, q, :T], p2[0:64, :T], 0, [kp2], [k4])
            stt(KTB[0:64, h, :T], KHP[:, q, :T], vec(0, "knn", 0, 0, 64), T4P[0:64, q, :T], ALU.mult, ALU.mult, [kkh, "VEC", k4], ["KTB"])
            cp(KTB[64:96, h, :T], krb[64:96, :T], [kkr], ["KTB"], eng="gpsimd")
        S.dma("sync", d_kS[slot, :, :, kcol:kcol + T].rearrange("h p c -> p h c"), KTB[:, :, :T], reads=["KTB"],
              writes=["d_kS%d" % slot])
        ps, kp = psb_full()
        for j in range(2):
            mm(ps[:T, :], latb[:, j, :T], WUV[:, j, :], ["WAR", klat], [kp], start=(j == 0), stop=(j == 1))
        cp(VTB[:T, :, 0:64], ps[:T, :].rearrange("p (h c) -> p h c", h=8), [kp], ["VTB"], eng="scalar")
        S.dma("sync", d_vS[slot, :, 0:T, vtile, :].rearrange("h p c -> p h c"), VTB[:T, :, :], reads=["VTB"],
              writes=["d_vS%d" % slot])

    def attention(T, slot, h, ktiles, kvpar, diag_tile):
        po, kpo = psq()
        n = len(ktiles)
        for i, (kcol, nk, vt) in enumerate(ktiles):
            ps, kp = psq()
            mm(ps[:nk, :T], KSB[:, kvpar, kcol:kcol + nk], QF[:, h, :T], ["KSB%d" % kvpar, "QF"], [kp])
            eb = EXB[:nk, i % 4, :T]
            ke = "EXB%d" % (i % 4)
            act(eb, ps[:nk, :T], AF.Exp, [kp], [ke], scale=96.0 ** -0.5)
            if i == diag_tile:
                tt(eb, eb, AMK[:nk, :T], ALU.mult, [ke, "AMK"], [ke], eng="gpsimd")
            mm(po[0:65, :T], VSB[:nk, kvpar, vt, :], eb, ["VSB%d" % kvpar, ke], [kpo], start=(i == 0), stop=(i == n - 1))
        cp(OSB[:, :T], po[0:65, :T], [kpo], ["OSB"], eng="scalar")
        pr, kpr = psq()
        mm(pr[0:64, :T], ONE65[64:65, :], OSB[64:65, :T], ["ONE65", "OSB"], [kpr])
        S.op("vector", lambda e: e.reciprocal(RCP[:, :T], pr[0:64, :T]), [kpr], ["RCP"])
        tt(OT[:, :T], OSB[0:64, :T], RCP[:, :T], ALU.mult, ["OSB", "RCP"], ["OT"])
        tt(YB[:, h, :T], OT[:, :T], SGB[:, h, :T], ALU.mult, ["OT", "SGB"], ["YB"])

    def phase_a_tile(l, par, T, tcol, xsrc, cst, segs, nseg, seglen, lat_out, kr_out, kv_slots):
        chk(1)
        norm_and_transpose(xsrc, T, par, tcol)
        chk(2)
        S.dma("sync", CST[:, :, :T], cst.rearrange("n p c -> p n c"), writes=["CST"])
        ps, kp = win_chunk(par, T, 0, 128)
        shift_mix(ps, kp, 0, T, nseg, seglen, XW[:, :T], "XW")
        act(TH[0:64, :T], XW[0:64, :T], AF.Tanh, ["XW"], ["TH"])
        chk(3)
        for hp in range(4):
            cur[2] = hp
            for j, nm in enumerate(("XR", "XK", "XV", "XG")):
                ci = 1 + 4 * hp + j
                ps, kp = win_chunk(par, T, ci * 128, 128)
                shift_mix(ps, kp, ci, T, nseg, seglen, RW[nm][:, :T], nm)
                chk(30 + ci)
            rwkv_hp(hp, T, segs)
            chk(4)
            chk(50 + hp)
        S.dma("sync", d_oa[:, :, tcol:tcol + T], OUTA[:, :, :T], reads=["OUTA"], writes=["d_oa%d" % tcol])
        for j in range(3):
            ps, kp = win_chunk(par, T, OFF_QC + j * 128, 128)
            cp(QC[:, j, :T], ps[:, :T], [kp], ["QC"], eng="scalar")
        rms_feat(QC, "QC", 3, O384, T, "qnorm", QC, "QC", QCN, "QCN")
        for j in range(2):
            ps, kp = win_chunk(par, T, OFF_CKV + j * 128, 128)
            cp(CKV[:, j, :T], ps[:, :T], [kp], ["CKV"], eng="scalar")
        rms_feat(CKV, "CKV", 2, O256, T, "kvnorm", LAT, "LAT", LATB, "LATB")
        for j in range(2):
            pt_, kt_ = psq()
            tr(pt_[:T, :], LAT[:, j, :T], IDENT, ["LAT", "CM"], [kt_])
            cp(LTM[:T, j * 128:(j + 1) * 128], pt_[:T, :], [kt_], ["LTM"], eng=("scalar" if j else "vector"))
        S.dma("sync", lat_out, LTM[:T, :], reads=["LTM"], writes=["o_lat"])
        ps, kp = win_chunk(par, T, OFF_KR, 96)
        cp(KR[64:96, :T], ps[64:96, :T], [kp], ["KR"], eng="scalar")
        act(T3[64:96, :T], ps[64:96, :T], AF.Square, [kp], ["T3"])
        p2, kp2 = psq()
        mm(p2[0:96, :T], OQ[64:96, 0:96], T3[64:96, :T], ["CM", "T3"], [kp2])
        rsq(T4[64:96, :T], p2[64:96, :T], 0, [kp2], ["T4"])
        stt(KRN[64:96, :T], KR[64:96, :T], vec(0, "knr", 0, 64, 96), T4[64:96, :T], ALU.mult, ALU.mult, ["KR", "VEC", "T4"], ["KRN"])
        p3, kp3 = psq()
        mm(p3[0:96, :T], ROT[64:96, 0:96], KRN[64:96, :T], ["CM", "KRN"], [kp3])
        tt(T3[64:96, :T], p3[64:96, :T], CST[64:96, 1, :T], ALU.mult, [kp3, "CST"], ["T3"])
        tt(KRF[64:96, :T], KRN[64:96, :T], CST[64:96, 0, :T], ALU.mult, ["KRN", "CST"], ["KRF"])
        tt(KRF[64:96, :T], KRF[64:96, :T], T3[64:96, :T], ALU.add, ["KRF", "T3"], ["KRF"])
        pt_, kt_ = psq()
        tr(pt_[:T, 0:96], KRF[0:96, :T], IDENT[0:96, 0:96], ["KRF", "CM"], [kt_])
        cp(KTM[:T, 64:96], pt_[:T, 64:96], [kt_], ["KTM"])
        S.dma("sync", kr_out, KTM[:T, 64:96], reads=["KTM"], writes=["o_kr"])
        for h in range(8):
            ps, kp = win_chunk(par, T, OFF_GB + h * 64, 64)
            act(SGB[:, h, :T], ps[0:64, :T], AF.Silu, [kp], ["SGB"])
        for h in range(8):
            ps, kp = psq()
            for j in range(3):
                mm(ps[0:96, :T], WUQ[:, j, h * 96:(h + 1) * 96], QCN[:, j, :T], ["WAR", "QCN"], [kp], start=(j == 0), stop=(j == 2))
            q = h % 2
            k3, k4, kqh, kqn = "T3P%d" % q, "T4P%d" % q, "QHP%d" % q, "QNP%d" % q
            cp(QHP[:, q, :T], ps[0:96, :T], [kp], [kqh], eng="scalar")
            act(T3P[0:96, q, :T], ps[0:96, :T], AF.Square, [kp], [k3])
            p2, kp2 = psq()
            mm(p2[0:96, :T], OQ[0:96, 0:96], T3P[0:96, q, :T], ["CM", k3], [kp2])
            rsq(T4P[0:96, q, :T], p2[0:96, :T], 0, [kp2], [k4])
            stt(QNP[:, q, :T], QHP[:, q, :T], vec(0, "qn", 0, 0, 96), T4P[0:96, q, :T], ALU.mult, ALU.mult, [kqh, "VEC", k4], [kqn])
            p3, kp3 = psq()
            mm(p3[0:96, :T], ROT[0:96, 0:96], QNP[:, q, :T], ["CM", kqn], [kp3])
            tt(T3P[0:96, q, :T], p3[0:96, :T], CST[0:96, 1, :T], ALU.mult, [kp3, "CST"], [k3])
            tt(T4P[0:96, q, :T], QNP[:, q, :T], CST[0:96, 0, :T], ALU.mult, [kqn, "CST"], [k4], eng="gpsimd")
            tt(QF[:, h, :T], T4P[0:96, q, :T], T3P[0:96, q, :T], ALU.add, [k3, k4], ["QF"])
        chk(5)
        if len(kv_slots) == 1:
            _, _, slot, kcol, vtile = kv_slots[0]
            expand_kv(LATB, "LATB", KRF, "KRF", T, slot, kcol, vtile)
        else:
            expand_kv_multi(T, kv_slots)

    def expand_kv_multi(T, kv_slots):
        for h in range(8):
            ps, kp = psq()
            for j in range(2):
                mm(ps[0:64, :T], WUK[:, j, h * 64:(h + 1) * 64], LATB[:, j, :T], ["WAR", "LATB"], [kp], start=(j == 0), stop=(j == 1))
            q = h % 2
            k3, k4, kkh = "T3P%d" % q, "T4P%d" % q, "KHP%d" % q
            cp(KHP[:, q, :T], ps[0:64, :T], [kp], [kkh], eng="scalar")
            act(T3P[0:64, q, :T], ps[0:64, :T], AF.Square, [kp], [k3])
            p2, kp2 = psq()
            mm(p2[0:64, :T], OB64[0:64, 0:64], T3P[0:64, q, :T], ["CM", k3], [kp2])
            rsq(T4P[0:64, q, :T], p2[0:64, :T], 0, [kp2], [k4])
            stt(KTB[0:64, h, :T], KHP[:, q, :T], vec(0, "knn", 0, 0, 64), T4P[0:64, q, :T], ALU.mult, ALU.mult, [kkh, "VEC", k4], ["KTB"])
            cp(KTB[64:96, h, :T], KRF[64:96, :T], ["KRF"], ["KTB"], eng="gpsimd")
        for (c0, ncol, slot, kcol, vtile) in kv_slots:
            S.dma("sync", d_kS[slot, :, :, kcol:kcol + ncol].rearrange("h p c -> p h c"), KTB[:, :, c0:c0 + ncol],
                  reads=["KTB"], writes=["d_kS%d" % slot])
            ps, kp = psb_full()
            for j in range(2):
                mm(ps[:ncol, :], LATB[:, j, c0:c0 + ncol], WUV[:, j, :], ["WAR", "LATB"], [kp], start=(j == 0), stop=(j == 1))
            cp(VTB[:ncol, :, 0:64], ps[:ncol, :].rearrange("p (h c) -> p h c", h=8), [kp], ["VTB"], eng="scalar")
            S.dma("sync", d_vS[slot, :, 0:ncol, vtile, :].rearrange("h p c -> p h c"), VTB[:ncol, :, :], reads=["VTB"],
                  writes=["d_vS%d" % slot])

    kv_rr = [0]

    def attend_seq(T, qcol0, slot, nkcols, ktiles, diag_tile, tcol):
        for h in range(8):
            par = kv_rr[0] % 2
            kv_rr[0] += 1
            S.dma("sync", KSB[:, par, 0:nkcols], d_kS[slot, h, :, 0:nkcols], reads=["d_kS%d" % slot], writes=["KSB%d" % par])
            full = [vt for _, nk, vt in ktiles if nk == 128]
            part = [(nk, vt) for _, nk, vt in ktiles if nk != 128]
            if full:
                v0, v1 = min(full), max(full) + 1
                S.dma("sync", VSB[:, par, v0:v1, :], d_vS[slot, h, :, v0:v1, :], reads=["d_vS%d" % slot], writes=["VSB%d" % par])
            for nk, vt in part:
                S.dma("sync", VSB[:nk, par, vt, :], d_vS[slot, h, 0:nk, vt, :], reads=["d_vS%d" % slot], writes=["VSB%d" % par])
            attention_cols(T, qcol0, slot, h, ktiles, par, diag_tile)

    def attention_cols(T, q0, slot, h, ktiles, kvpar, diag_tile):
        po, kpo = psacc()
        n = len(ktiles)
        for i, (kcol, nk, vt) in enumerate(ktiles):
            ps, kp = psq()
            mm(ps[:nk, :T], KSB[:, kvpar, kcol:kcol + nk], QF[:, h, q0:q0 + T], ["KSB%d" % kvpar, "QF"], [kp])
            eb = EXB[:nk, i % 4, :T]
            ke = "EXB%d" % (i % 4)
            act(eb, ps[:nk, :T], AF.Exp, [kp], [ke], scale=96.0 ** -0.5)
            if i == diag_tile:
                tt(eb, eb, AMK[:nk, :T], ALU.mult, [ke, "AMK"], [ke], eng="gpsimd")
            mm(po[0:65, :T], VSB[:nk, kvpar, vt, :], eb, ["VSB%d" % kvpar, ke], [kpo], start=(i == 0), stop=(i == n - 1))
        cp(OSB[:, :T], po[0:65, :T], [kpo], ["OSB"], eng="scalar")
        pr, kpr = psq()
        mm(pr[0:64, :T], ONE65[64:65, :], OSB[64:65, :T], ["ONE65", "OSB"], [kpr])
        S.op("vector", lambda e: e.reciprocal(RCP[:, :T], pr[0:64, :T]), [kpr], ["RCP"])
        tt(OT[:, :T], OSB[0:64, :T], RCP[:, :T], ALU.mult, ["OSB", "RCP"], ["OT"])
        tt(YB[:, h, q0:q0 + T], OT[:, :T], SGB[:, h, q0:q0 + T], ALU.mult, ["OT", "SGB"], ["YB"])

    def phase_b_tile(l, T, tcol, xsrc, xdst):
        par = 0
        S.dma("sync", HT[:, par, :, :T], d_hT[:, :, tcol:tcol + T], reads=["d_hT%d" % tcol], writes=["HT%d" % par])
        S.dma("sync", OUTA[:, :, :T], d_oa[:, :, tcol:tcol + T], reads=["d_oa%d" % tcol], writes=["OUTA"])
        S.dma("sync", YB[:, :, :T], d_yb[:, :, tcol:tcol + T], reads=["d_yb%d" % tcol], writes=["YB"])
        S.dma("sync", XT[:T, 0, :], xsrc, reads=["d_xres%d" % tcol], writes=["XT0"])
        for j in range(8):
            pa_, ka_ = psq()
            for k in range(4):
                mm(pa_[:, :T], WA[:, k, j * 128:(j + 1) * 128], OUTA[:, k, :T], ["WAR", "OUTA"], [ka_], start=(k == 0), stop=(k == 3))
            pb_, kb_ = psq()
            for h in range(8):
                mm(pb_[:, :T], WB[0:64, h, j * 128:(j + 1) * 128], YB[:, h, :T], ["WAR", "YB"], [kb_], start=(h == 0), stop=(h == 7))
            for g in range(2):
                pg, kg = psq()
                for kc in range(8):
                    mm(pg[:, :T], WING[:, kc, (2 * j + g) * 128:(2 * j + g + 1) * 128], HT[:, par, kc, :T], ["WAR", "HT%d" % par], [kg],
                       start=(kc == 0), stop=(kc == 7))
                act(GAB[:, g, :T], pg[:, :T], AF.Sigmoid, [kg], ["GAB%d" % g])
            tt(T3[:, :T], pa_[:, :T], GAB[:, 0, :T], ALU.mult, [ka_, "GAB0"], ["T3"])
            tt(T4[:, :T], pb_[:, :T], GAB[:, 1, :T], ALU.mult, [kb_, "GAB1"], ["T4"])
            tt(MM[:, j, :T], T3[:, :T], T4[:, :T], ALU.add, ["T3", "T4"], ["MM"], eng="gpsimd")
        for c in range(2):
            po, kpo = psb_full()
            for kc in range(8):
                mm(po[:T, :], MM[:, kc, :T], WOUT[:, kc, c * 512:(c + 1) * 512], ["WAR", "MM"], [kpo], start=(kc == 0), stop=(kc == 7))
            tt(XT[:T, 1, c * 512:(c + 1) * 512], po[:T, :], XT[:T, 0, c * 512:(c + 1) * 512], ALU.add, [kpo, "XT0"], ["XT1"])
        for dst, kd in xdst:
            S.dma("sync", dst, XT[:T, 1, :], reads=["XT1"], writes=[kd])

    def load_w(dst, src, kchunks, key="WAR"):
        S.dma("sync", dst, src.rearrange("(k p) c -> p k c", p=128), reads=["wcast"], writes=[key])

    ptiles = [(NMETA, 0)] + [(128, NMETA + 128 * i) for i in range(NPT)]
    SCOL = L

    try:
      chk(0)
      for l in range(DEPTH):
          last = (l == DEPTH - 1)
          chk(100 + 10 * l)
          S.dma("sync", VEC[:], i_vecs[l], writes=["VEC"])
          S.dma("sync", NRW[:], i_normw[l], writes=["NRW"])
          S.dma("sync", W2A2[:], i_w2a2[l], writes=["W2A2"])
          S.dma("sync", WIN, wb_win_ab[l].rearrange("(k p) c -> p k c", p=128), reads=["b_win_ab"], writes=["WAR"])
          S.dma("sync", WUQ, wb_wuq[l].rearrange("(k p) c -> p k c", p=128), reads=["b_wuq"], writes=["WAR"])
          S.dma("sync", WUK, wb_wuk[l].rearrange("(k p) c -> p k c", p=128), reads=["b_wuk"], writes=["WAR"])
          S.dma("sync", WUV, wb_wuv[l].rearrange("(k p) c -> p k c", p=128), reads=["b_wuv"], writes=["WAR"])
          for ci in range(17):
              ms(CAR[:, ci, :], 0.0, ["CAR%d" % ci])
          for hp in range(4):
              ms(ST[:, 0, hp, :], 0.0, ["ST0_%d" % hp])
          for ti, (T, tcol) in enumerate(ptiles):
              if l == 0:
                  xsrc = i_meta if ti == 0 else i_xp[tcol - NMETA:tcol - NMETA + T, :]
              else:
                  xsrc = d_xres[tcol:tcol + T, :]
              par = ti % 2
              cur[0], cur[1] = l, ti
              chk(200 + 20 * l + ti)
              phase_a_tile(l, par, T, tcol, xsrc, i_csp[:, :, tcol:tcol + T], [(0, T, 0)], 1, T,
                           o_latp[l, tcol:tcol + T, :], o_krp[l, tcol:tcol + T, :], [(0, T, 0, tcol, ti)])
              if ti == 0:
                  ktiles = [(0, NMETA, 0)]
                  diag = -1
              else:
                  ktiles = [(0, NMETA, 0)] + [(NMETA + 128 * i, 128, 1 + i) for i in range(ti)]
                  diag = ti
              chk(6)
              attend_seq(T, 0, 0, tcol + T, ktiles, diag, tcol)
              chk(7)
              S.dma("sync", d_yb[:, :, tcol:tcol + T], YB[:, :, :T], reads=["YB"], writes=["d_yb%d" % tcol])
          chk(100 + 10 * l + 1)
          for hp in range(4):
              for h in range(2):
                  S.dma("sync", o_srp[l, 2 * hp + h].rearrange("v k -> k v"), ST[64 * h:64 * h + 64, 0, hp, :],
                        reads=["ST0_%d" % hp], writes=["o_srp"], slow=True)
          for ci in range(17):
              s0 = a_src(ci)
              S.dma("sync", o_shp[l, s0:s0 + 128].rearrange("(p o) -> p o", o=1), CAR[:, ci, 0:1], reads=["CAR%d" % ci],
                    writes=["o_shp"], slow=True)
          chk(100 + 10 * l + 2)
          for b in range(4):
              slot = 1 + b
              for t0 in range(0, PAST, 512):
                  for s in range(4):
                      S.dma("sync", PLT[:, s, :], i_clat[l, b, t0 + s * 128:t0 + (s + 1) * 128, :], writes=["PLT"])
                      S.dma("sync", PKT[:, s, 64:96], i_ckr[l, b, t0 + s * 128:t0 + (s + 1) * 128, :], writes=["PKT"])
                  for s in range(4):
                      for j in range(2):
                          pt_, kt_ = psq()
                          tr(pt_[:, :], PLT[:, s, j * 128:(j + 1) * 128], IDENT, ["PLT", "CM"], [kt_])
                          cp(LPB[:, j, s * 128:(s + 1) * 128], pt_[:, :], [kt_], ["LPB"], eng=("scalar" if j else "vector"))
                      pt_, kt_ = psq()
                      tr(pt_[0:96, :], PKT[:, s, :], IDENT, ["PKT", "CM"], [kt_])
                      cp(KPF[64:96, s * 128:(s + 1) * 128], pt_[64:96, :], [kt_], ["KPF"], eng="scalar")
                  for s in range(4):
                      expand_kv(LPB[:, :, s * 128:(s + 1) * 128], "LPB", KPF[:, s * 128:(s + 1) * 128], "KPF", 128, slot,
                                t0 + s * 128, (t0 // 128) + s)
          chk(100 + 10 * l + 3)
          for ci in range(17):
              s0 = a_src(ci)
              S.dma("sync", CAR[:, ci, 0:4], i_stsh[l, :, s0:s0 + 128].rearrange("b p -> p b"), writes=["CAR%d" % ci], slow=True)
          for b in range(4):
              for hp in range(4):
                  for h in range(2):
                      S.dma("sync", ST[64 * h:64 * h + 64, 1 + b, hp, :], i_strw[l, b, 2 * hp + h].rearrange("v k -> k v"),
                            writes=["ST%d_%d" % (1 + b, hp)], slow=True)
          xsrc = i_xs if l == 0 else d_xres[SCOL:SCOL + 64, :]
          phase_a_tile(l, 0, 64, SCOL, xsrc, i_css, [(16 * b, 16, 1 + b) for b in range(4)], 4, 16,
                       o_latn[l], o_krn[l], [(16 * b, 16, 1 + b, PAST, PAST // 128) for b in range(4)])
          for b in range(4):
              ktiles = [(128 * i, 128, i) for i in range(PAST // 128)] + [(PAST, 16, PAST // 128)]
              attend_seq(16, 16 * b, 1 + b, PAST + 16, ktiles, -1, SCOL)
          S.dma("sync", d_yb[:, :, SCOL:SCOL + 64], YB[:, :, :64], reads=["YB"], writes=["d_yb%d" % SCOL])
          for b in range(4):
              for hp in range(4):
                  for h in range(2):
                      S.dma("sync", o_srs[l, b, 2 * hp + h].rearrange("v k -> k v"), ST[64 * h:64 * h + 64, 1 + b, hp, :],
                            reads=["ST%d_%d" % (1 + b, hp)], writes=["o_srs"], slow=True)
          for ci in range(17):
              s0 = a_src(ci)
              S.dma("sync", o_shs[l, :, s0:s0 + 128].rearrange("b p -> p b"), CAR[:, ci, 0:4], reads=["CAR%d" % ci],
                    writes=["o_shs"], slow=True)
          chk(100 + 10 * l + 4)
          S.dma("sync", WING, wb_win_g[l].rearrange("(k p) c -> p k c", p=128), reads=["b_win_g"], writes=["WAR"])
          S.dma("sync", WA, wb_wa[l].rearrange("(k p) c -> p k c", p=128), reads=["b_wa"], writes=["WAR"])
          S.dma("sync", WB[0:64], wb_wb[l].rearrange("(k p) c -> p k c", p=64), reads=["b_wb"], writes=["WAR"])
          S.dma("sync", WOUT, wb_wout[l].rearrange("(k p) c -> p k c", p=128), reads=["b_wout"], writes=["WAR"])
          for ti, (T, tcol) in enumerate(ptiles):
              if l == 0:
                  xsrc = i_meta if ti == 0 else i_xp[tcol - NMETA:tcol - NMETA + T, :]
              else:
                  xsrc = d_xres[tcol:tcol + T, :]
              dsts = []
              if not last:
                  dsts.append((d_xres[tcol:tcol + T, :], "d_xres%d" % tcol))
              elif ti > 0:
                  dsts.append((o_yp[tcol - NMETA:tcol - NMETA + T, :], "o_yp"))
              phase_b_tile(l, T, tcol, xsrc, dsts)
          xsrc = i_xs if l == 0 else d_xres[SCOL:SCOL + 64, :]
          phase_b_tile(l, 64, SCOL, xsrc, [(o_ys, "o_ys")] if last else [(d_xres[SCOL:SCOL + 64, :], "d_xres%d" % SCOL)])

    except StopBuild:
        pass
    if cfg.get("VERBOSE"):
        print("sbuf bytes remaining", nc.sbuf_bytes_remaining)
    n_ops = S.emit(nc)
    if cfg.get("VERBOSE"):
        print("sched stats", {k: v[:2] for k, v in S.stats.items()}, S.stats["sync"][2], S.stats["sync"][3])
    es.close()
    return nc, n_ops


_CACHE = {}


def run(inputs, cfg):
    maps = host_layout(inputs, cfg)
    nc, n_ops = build(cfg)
    res = run_bass_kernel_spmd(nc, maps, core_ids=list(range(8)))
    return res.results


def kernel(**inputs):
    cfg = dict(DEPTH=4, SEQ=4096, PAST=2048)
    r = run(inputs, cfg)
    return assemble(r, cfg)


def assemble(r, cfg):
    DEPTH, SEQ, PAST = cfg["DEPTH"], cfg["SEQ"], cfg["PAST"]
    f = np.float32
    y_p = np.stack([np.asarray(r[c]["y_p"], f) for c in range(8)])
    y_s = np.concatenate([np.asarray(r[c]["y_s"], f).reshape(4, 16, D) for c in range(8)])
    srp = np.stack([np.asarray(r[c]["o_srp"], f) for c in range(8)], axis=1)
    shp = np.stack([np.asarray(r[c]["o_shp"], f) for c in range(8)], axis=1)[:, :, None, :]
    latp = np.stack([np.asarray(r[c]["o_latp"], f) for c in range(8)], axis=1)
    krp = np.stack([np.asarray(r[c]["o_krp"], f) for c in range(8)], axis=1)
    srs = np.concatenate([np.asarray(r[c]["o_srs"], f) for c in range(8)], axis=1)
    shs = np.concatenate([np.asarray(r[c]["o_shs"], f) for c in range(8)], axis=1)[:, :, None, :]
    latn = np.concatenate([np.asarray(r[c]["o_latn"], f).reshape(DEPTH, 4, 16, 256) for c in range(8)], axis=1)
    krn = np.concatenate([np.asarray(r[c]["o_krn"], f).reshape(DEPTH, 4, 16, 32) for c in range(8)], axis=1)
    return (y_p, y_s, srp, shp, latp, krp, srs, shs, latn, krn)
```
